# Optimizing a Trainium2 kernel written in Bass

```python
import jax, jax.numpy as jnp
from jax import lax
import numpy as np

D_MODEL = 1024
BATCH = 4
SEQ = 8192
DEPTH = 2
DEC_BATCH = 16
DEC_SEQ = 32
PAST_LEN = 2048

CHUNK = 64
GLA_H = 6
GLA_DK = 32
GLA_DV = 64
GLA_RANK = 16
GLA_GATE_NORM = 16.0
HG_H = 5
HG_DK = 64
HG_DV = 64
ML_H = 5
ML_DK = 64
ML_DV = 64
ML_CONV = 4
D_MIX = GLA_H * GLA_DV + HG_H * HG_DV + ML_H * ML_DV
D_FF = 2816
FFN_CONV = 3
EPS = 1e-6
NEG_BIG = -1e30

SPLIT_SIZES = (GLA_H * GLA_DK, GLA_H * GLA_DK, GLA_H * GLA_DV, GLA_RANK, GLA_H * GLA_DV,
               HG_H * HG_DK, HG_H * HG_DK, HG_H * HG_DV, HG_H * HG_DV,
               ML_H * ML_DK, ML_H * ML_DK, ML_H * ML_DV, ML_H, ML_H, ML_H * ML_DV)
N_IN = sum(SPLIT_SIZES)

kernel_name = 'hybrid_gla_hgrn2_mlstm_stream_step'


def rms_norm(x, g):
    xf = x.astype(jnp.float32)
    y = xf * lax.rsqrt(jnp.mean(xf * xf, axis=-1, keepdims=True) + EPS)
    return (y * g.astype(jnp.float32)).astype(x.dtype)


def head_rms_norm(x, g):
    h, d = x.shape[-2], x.shape[-1]
    y = x * lax.rsqrt(jnp.mean(x * x, axis=-1, keepdims=True) + EPS)
    return y * g.reshape(h, d)


def causal_dwconv(x, buf, w, b):
    width, t = w.shape[0], x.shape[1]
    xp = jnp.concatenate([buf.astype(x.dtype), x], axis=1)
    y = b
    for j in range(width):
        y = y + xp[:, j:j + t] * w[j]
    return y, xp[:, t:]


def _to_chunks(a, L):
    b, t = a.shape[0], a.shape[1]
    return a.reshape((b, t // L, L) + a.shape[2:]).swapaxes(0, 1)


def _from_chunks(a):
    n, b, L = a.shape[0], a.shape[1], a.shape[2]
    return a.swapaxes(0, 1).reshape((b, n * L) + a.shape[3:])


def gated_linear_scan(q, k, v, log_a, s0, L):
    mask = jnp.tril(jnp.ones((L, L), dtype=bool))[None, :, :, None, None]

    def step(s, inp):
        qc, kc, vc, ac = inp
        b = jnp.cumsum(ac, axis=1)
        diff = b[:, :, None] - b[:, None, :]
        decay = jnp.where(mask, jnp.exp(jnp.where(mask, diff, 0.0)), 0.0)
        scores = jnp.einsum('bihd,bjhd,bijhd->bijh', qc, kc, decay)
        o = (jnp.einsum('bijh,bjhv->bihv', scores, vc)
             + jnp.einsum('bihk,bhkv->bihv', qc * jnp.exp(b), s))
        b_last = b[:, -1]
        s_new = (jnp.exp(b_last)[..., None] * s
                 + jnp.einsum('bjhk,bjhv->bhkv', kc * jnp.exp(b_last[:, None] - b), vc))
        return s_new, o

    s_fin, o = lax.scan(step, s0, tuple(_to_chunks(a, L) for a in (q, k, v, log_a)))
    return _from_chunks(o), s_fin


def mlstm_scan(q, k, v, log_f, ig, c0, n0, m0, L):
    mask = jnp.tril(jnp.ones((L, L), dtype=bool))[None, :, :, None]

    def step(carry, inp):
        c, n, m = carry
        qc, kc, vc, fc, ic = inp
        b = jnp.cumsum(fc, axis=1)
        raw = b[:, :, None] - b[:, None, :] + ic[:, None, :]
        dlog = jnp.where(mask, raw, NEG_BIG)
        g = b + m[:, None]
        m_i = jnp.maximum(g, jnp.max(dlog, axis=2))
        w = jnp.where(mask, jnp.exp(dlog - m_i[:, :, None]), 0.0)
        w0 = jnp.exp(g - m_i)
        s = jnp.einsum('bihd,bjhd->bijh', qc, kc) * w
        num = (jnp.einsum('bijh,bjhv->bihv', s, vc)
               + w0[..., None] * jnp.einsum('bihk,bhkv->bihv', qc, c))
        den = jnp.sum(s, axis=2) + w0 * jnp.einsum('bihk,bhk->bih', qc, n)
        h = num / jnp.maximum(jnp.abs(den), jnp.exp(-m_i))[..., None]
        m_last = m_i[:, -1]
        wk = jnp.exp(b[:, -1:] - b + ic - m_last[:, None])
        dc = jnp.exp(b[:, -1] + m - m_last)
        c_new = dc[..., None, None] * c + jnp.einsum('bjh,bjhk,bjhv->bhkv', wk, kc, vc)
        n_new = dc[..., None] * n + jnp.einsum('bjh,bjhk->bhk', wk, kc)
        return (c_new, n_new, m_last), h

    (c, n, m), h = lax.scan(step, (c0, n0, m0),
                            tuple(_to_chunks(a, L) for a in (q, k, v, log_f, ig)))
    return _from_chunks(h), (c, n, m)


def mixer(xn, st, lb, w_in, gla_w_gate, gla_b_gate, ml_conv_w, ml_conv_b, ml_b_i, ml_b_f,
          g_head, w_out):
    s_gla, s_hg, c_ml, n_ml, m_ml, buf_ml = st
    B, T, _ = xn.shape
    L = CHUNK if T % CHUNK == 0 else T
    f32 = jnp.float32
    pts, acc = [], 0
    for sz in SPLIT_SIZES[:-1]:
        acc += sz
        pts.append(acc)
    proj = xn @ w_in
    (gq, gk, gv, gg, gr, hq, hf, hi, hg, mq, mk, mv, mi, mf, mo) = jnp.split(proj, pts, axis=-1)

    def hs(a, h):
        return a.astype(f32).reshape(B, T, h, -1)

    g_a, g_b, g_c = jnp.split(g_head.astype(f32), [GLA_H * GLA_DV, GLA_H * GLA_DV + HG_H * HG_DV])

    q = hs(gq, GLA_H) * GLA_DK ** -0.5
    log_a = jax.nn.log_sigmoid((gg @ gla_w_gate + gla_b_gate).astype(f32)).reshape(
        B, T, GLA_H, GLA_DK) / GLA_GATE_NORM
    o_a, s_gla = gated_linear_scan(q, hs(gk, GLA_H), hs(gv, GLA_H), log_a, s_gla.astype(f32), L)
    o_a = head_rms_norm(o_a, g_a) * jax.nn.silu(hs(gr, GLA_H))

    lbh = lb.astype(f32).reshape(HG_H, HG_DK)
    fr = hs(hf, HG_H)
    log_f = jnp.log(lbh + (1.0 - lbh) * jax.nn.sigmoid(fr))
    k = (1.0 - lbh) * jax.nn.sigmoid(-fr)
    q = jax.nn.silu(hs(hq, HG_H)) * HG_DK ** -0.5
    o_b, s_hg = gated_linear_scan(q, k, hs(hi, HG_H), log_f, s_hg.astype(f32), L)
    o_b = head_rms_norm(o_b, g_b) * jax.nn.silu(hs(hg, HG_H))

    qk, buf_ml = causal_dwconv(jnp.concatenate([mq, mk], axis=-1), buf_ml, ml_conv_w, ml_conv_b)
    qk = jax.nn.silu(qk.astype(f32))
    q = qk[..., :ML_H * ML_DK].reshape(B, T, ML_H, ML_DK)
    k = qk[..., ML_H * ML_DK:].reshape(B, T, ML_H, ML_DK) * ML_DK ** -0.5
    ig = mi.astype(f32) + ml_b_i.astype(f32)
    log_fm = jax.nn.log_sigmoid(mf.astype(f32) + ml_b_f.astype(f32))
    h, (c_ml, n_ml, m_ml) = mlstm_scan(q, k, hs(mv, ML_H), log_fm, ig, c_ml.astype(f32),
                                       n_ml.astype(f32), m_ml.astype(f32), L)
    o_c = head_rms_norm(h, g_c) * jax.nn.sigmoid(hs(mo, ML_H))

    cat = jnp.concatenate([o_a.reshape(B, T, -1), o_b.reshape(B, T, -1), o_c.reshape(B, T, -1)],
                          axis=-1).astype(xn.dtype)
    return cat @ w_out, (s_gla, s_hg, c_ml, n_ml, m_ml, buf_ml)


def conv_ffn(xn, buf, w_up, conv_w, conv_b, w_down):
    up = xn @ w_up
    up, buf = causal_dwconv(up, buf, conv_w, conv_b)
    gate, val = jnp.split(up, 2, axis=-1)
    return (jax.nn.gelu(gate, approximate=True) * val) @ w_down, buf


def run_trunk(x, states, params):
    (g_mix_pre, g_mix_post, g_ffn_pre, g_ffn_post, w_in, gla_w_gate, gla_b_gate, hgrn_lb,
     ml_conv_w, ml_conv_b, ml_b_i, ml_b_f, g_head, w_out, ffn_w_up, ffn_conv_w, ffn_conv_b,
     ffn_w_down) = params
    sm = jax.nn.softmax(hgrn_lb.astype(jnp.float32), axis=0)
    lb_all = jnp.cumsum(sm, axis=0) - sm[0:1]
    new = [[] for _ in range(7)]
    for l in range(DEPTH):
        st = tuple(s[l] for s in states[:6])
        h, st_new = mixer(rms_norm(x, g_mix_pre[l]), st, lb_all[l], w_in[l], gla_w_gate[l],
                          gla_b_gate[l], ml_conv_w[l], ml_conv_b[l], ml_b_i[l], ml_b_f[l],
                          g_head[l], w_out[l])
        x = x + rms_norm(h, g_mix_post[l])
        h, ffn_buf = conv_ffn(rms_norm(x, g_ffn_pre[l]), states[6][l], ffn_w_up[l], ffn_conv_w[l],
                              ffn_conv_b[l], ffn_w_down[l])
        x = x + rms_norm(h, g_ffn_post[l])
        for i, s in enumerate(st_new + (ffn_buf,)):
            new[i].append(s)
    return x, [jnp.stack(s, axis=0) for s in new]


def setup_inputs(seed: int = 0) -> dict:
    key = jax.random.key(seed)
    ks = jax.random.split(key, 32)

    def nrm(k, shape, s):
        return s * jax.random.normal(k, shape, jnp.float32)

    def gain(k, shape):
        return 1.0 + 0.05 * jax.random.normal(k, shape, jnp.float32)

    D = D_MODEL
    return {
        'x_prompt': nrm(ks[0], (BATCH, SEQ, D), 1.0),
        'x_sample': nrm(ks[1], (DEC_BATCH, DEC_SEQ, D), 1.0),
        'state_gla': nrm(ks[2], (DEPTH, DEC_BATCH, GLA_H, GLA_DK, GLA_DV), 0.5),
        'state_hgrn': nrm(ks[3], (DEPTH, DEC_BATCH, HG_H, HG_DK, HG_DV), 1.0),
        'state_mlstm_C': nrm(ks[4], (DEPTH, DEC_BATCH, ML_H, ML_DK, ML_DV), 0.1),
        'state_mlstm_n': nrm(ks[5], (DEPTH, DEC_BATCH, ML_H, ML_DK), 0.1),
        'state_mlstm_m': nrm(ks[6], (DEPTH, DEC_BATCH, ML_H), 1.0),
        'cache_mlstm_conv': nrm(ks[7], (DEPTH, DEC_BATCH, ML_CONV - 1, 2 * ML_H * ML_DK), 1.0),
        'cache_ffn_conv': nrm(ks[8], (DEPTH, DEC_BATCH, FFN_CONV - 1, 2 * D_FF), 1.0),
        'g_mix_pre': gain(ks[9], (DEPTH, D)),
        'g_mix_post': gain(ks[10], (DEPTH, D)),
        'g_ffn_pre': gain(ks[11], (DEPTH, D)),
        'g_ffn_post': gain(ks[12], (DEPTH, D)),
        'w_in': nrm(ks[13], (DEPTH, D, N_IN), D ** -0.5),
        'gla_w_gate': nrm(ks[14], (DEPTH, GLA_RANK, GLA_H * GLA_DK), GLA_RANK ** -0.5),
        'gla_b_gate': nrm(ks[15], (DEPTH, GLA_H * GLA_DK), 0.1),
        'hgrn_lb': nrm(ks[16], (DEPTH, HG_H * HG_DK), 0.1),
        'ml_conv_w': nrm(ks[17], (DEPTH, ML_CONV, 2 * ML_H * ML_DK), ML_CONV ** -0.5),
        'ml_conv_b': nrm(ks[18], (DEPTH, 2 * ML_H * ML_DK), 0.01),
        'ml_b_i': nrm(ks[19], (DEPTH, ML_H), 0.1),
        'ml_b_f': jnp.linspace(3.0, 6.0, ML_H, dtype=jnp.float32)[None, :] + nrm(ks[20], (DEPTH, ML_H), 0.1),
        'g_head': gain(ks[21], (DEPTH, D_MIX)),
        'w_out': nrm(ks[22], (DEPTH, D_MIX, D), D_MIX ** -0.5),
        'ffn_w_up': nrm(ks[23], (DEPTH, D, 2 * D_FF), D ** -0.5),
        'ffn_conv_w': nrm(ks[24], (DEPTH, FFN_CONV, 2 * D_FF), FFN_CONV ** -0.5),
        'ffn_conv_b': nrm(ks[25], (DEPTH, 2 * D_FF), 0.01),
        'ffn_w_down': nrm(ks[26], (DEPTH, D_FF, D), D_FF ** -0.5),
    }


def reference(x_prompt, x_sample, state_gla, state_hgrn, state_mlstm_C, state_mlstm_n,
              state_mlstm_m, cache_mlstm_conv, cache_ffn_conv, g_mix_pre, g_mix_post, g_ffn_pre,
              g_ffn_post, w_in, gla_w_gate, gla_b_gate, hgrn_lb, ml_conv_w, ml_conv_b, ml_b_i,
              ml_b_f, g_head, w_out, ffn_w_up, ffn_conv_w, ffn_conv_b, ffn_w_down):
    params = (g_mix_pre, g_mix_post, g_ffn_pre, g_ffn_post, w_in, gla_w_gate, gla_b_gate, hgrn_lb,
              ml_conv_w, ml_conv_b, ml_b_i, ml_b_f, g_head, w_out, ffn_w_up, ffn_conv_w,
              ffn_conv_b, ffn_w_down)
    f32 = jnp.float32
    bp = x_prompt.shape[0]
    zero_states = (
        jnp.zeros((DEPTH, bp, GLA_H, GLA_DK, GLA_DV), f32),
        jnp.zeros((DEPTH, bp, HG_H, HG_DK, HG_DV), f32),
        jnp.zeros((DEPTH, bp, ML_H, ML_DK, ML_DV), f32),
        jnp.zeros((DEPTH, bp, ML_H, ML_DK), f32),
        jnp.zeros((DEPTH, bp, ML_H), f32),
        jnp.zeros((DEPTH, bp, ML_CONV - 1, 2 * ML_H * ML_DK), x_prompt.dtype),
        jnp.zeros((DEPTH, bp, FFN_CONV - 1, 2 * D_FF), x_prompt.dtype),
    )
    y_prompt, ps = run_trunk(x_prompt, zero_states, params)
    sample_states = (state_gla, state_hgrn, state_mlstm_C, state_mlstm_n, state_mlstm_m,
                     cache_mlstm_conv, cache_ffn_conv)
    y_sample, ss = run_trunk(x_sample, sample_states, params)
    return (y_prompt, y_sample, ps[0], ps[1], ps[2], ps[3], ps[4], ps[5], ps[6],
            ss[0], ss[1], ss[2], ss[3], ss[4], ss[5], ss[6])
```

```python
import os
import sys
import numpy as np
import concourse.bass as bass
import concourse.mybir as mybir
from concourse.bass_utils import run_bass_kernel_spmd

F32 = mybir.dt.float32
BF16 = mybir.dt.bfloat16
AF = mybir.ActivationFunctionType
ALU = mybir.AluOpType
AX = mybir.AxisListType

D = 1024
DFF = 2816
EPS = 1e-6
OFF = dict(gq=0, gk=192, gv=384, gg=768, gr=784, hq=1168, hf=1488, hi=1808, hg=2128,
           mq=2448, mk=2768, mv=3088, mi=3408, mf=3413, mo=3418)
LN8 = float(np.log(0.125))

FM_TILES = []
for t in range(2):
    FM_TILES.append(("qA%d" % t, 96, [(0, OFF['gq'] + 96 * t, 96)]))
for t in range(2):
    FM_TILES.append(("kA%d" % t, 96, [(0, OFF['gk'] + 96 * t, 96)]))
FM_TILES.append(("gg", 16, [(0, OFF['gg'], 16)]))
BR = [128, 128, 64]
for nm, src in (("qB", 'hq'), ("fB", 'hf'), ("qC", 'mq'), ("kC", 'mk')):
    for t in range(3):
        FM_TILES.append(("%s%d" % (nm, t), BR[t], [(0, OFF[src] + 128 * t, BR[t])]))
FM_TILES.append(("GI", 69, [(0, OFF['mi'], 5), (32, OFF['mi'], 5), (64, OFF['mi'], 5)]))
FM_TILES.append(("GF", 69, [(0, OFF['mf'], 5), (32, OFF['mf'], 5), (64, OFF['mf'], 5)]))
FM_PIECES = [["qA0", "qA1", "kA0", "kA1", "gg"], ["qB0", "qB1", "qB2", "fB0"],
             ["fB1", "fB2", "qC0", "qC1"], ["qC2", "kC0", "kC1", "kC2"], ["GI", "GF"]]
FM_BY = {n: (r, s) for n, r, s in FM_TILES}
TM_COLS = (list(range(OFF['gv'], OFF['gv'] + 384)) + list(range(OFF['hi'], OFF['hi'] + 320)) +
           list(range(OFF['mv'], OFF['mv'] + 320)) + list(range(OFF['gr'], OFF['gr'] + 384)) +
           list(range(OFF['hg'], OFF['hg'] + 320)) + list(range(OFF['mo'], OFF['mo'] + 320)))
SLOT = 5632


def piece_table():
    P = []
    for i, names in enumerate(FM_PIECES):
        cols = []
        tiles = []
        for n in names:
            r, srcs = FM_BY[n]
            c = -np.ones(r, np.int64)
            for (dr, sc, nn) in srcs:
                c[dr:dr + nn] = np.arange(sc, sc + nn)
            tiles.append((n, len(cols), r))
            cols += list(c)
        P.append(dict(name="fm%d" % i, src='in', cols=np.array(cols), kcs=list(range(8)), gain=0, tiles=tiles))
    for i in range(4):
        P.append(dict(name="tm%d" % i, src='in', cols=np.array(TM_COLS[512 * i:512 * i + 512]), kcs=list(range(8)), gain=0))
    for i in range(2):
        P.append(dict(name="out%d" % i, src='out', cols=np.arange(512 * i, 512 * i + 512), kcs=list(range(8)), gain=None))
    for i in range(11):
        cols = []
        for g in (2 * i, 2 * i + 1):
            cols += list(range(128 * g, 128 * g + 128)) + list(range(DFF + 128 * g, DFF + 128 * g + 128))
        P.append(dict(name="up%d" % i, src='up', cols=np.array(cols), kcs=list(range(8)), gain=1))
    for cg in range(2):
        for kh in range(2):
            P.append(dict(name="dn%d%d" % (cg, kh), src='dn', cols=np.arange(512 * cg, 512 * cg + 512),
                          kcs=list(range(11 * kh, 11 * kh + 11)), gain=None))
    off = 0
    for p in P:
        p['nk'] = len(p['kcs'])
        p['nc'] = len(p['cols'])
        p['off'] = off
        p['size'] = p['nk'] * p['nc']
        assert p['size'] <= SLOT
        off += p['size']
    return P, off


PIECES, NTOT = piece_table()
PIDX = {p['name']: i for i, p in enumerate(PIECES)}

CP_BG = 0
CP_CW = 2
CP_BI = 32
CP_BF = 33
CP_FF = 34
CP_N = 34 + 176


class V:
    def __init__(s, ap, key):
        s.ap = ap
        s.key = key

    def __getitem__(s, idx):
        return V(s.ap[idx], s.key)

    def k(s, sub):
        return V(s.ap, (s.key, sub))

    def re(s, pat, **kw):
        return V(s.ap.rearrange(pat, **kw), s.key)

    def bc(s, shape):
        return V(s.ap.broadcast_to(list(shape)), s.key)

    def bitcast(s, dt):
        return V(s.ap.bitcast(dt), s.key)


class Rec:
    __slots__ = ("eng", "fn", "deps", "signal", "dma", "cnt", "line")

    def __init__(s, eng, fn):
        s.eng = eng
        s.fn = fn
        s.deps = []
        s.signal = False
        s.dma = None
        s.cnt = 0


ENGS = ["pe", "act", "dve", "pool", "sp"]
DEBUG_H = []
import itertools
UNIQ = itertools.count()
TRACE_LINES = bool(os.environ.get('TRACE_LINES'))
WAITLOG = []
NDMASEM = 24


class Gen:
    def __init__(s, nc):
        s.nc = nc
        s.q = {e: [] for e in ENGS}
        s.bufs = {}
        s.ndma = 0
        s.npool = 0
        s.dma_uses = [0] * NDMASEM

    def _dep(s, eng, reads, writes):
        deps = set()
        for kk in reads:
            b = s.bufs.get(kk)
            if b and b[0] is not None:
                deps.add(b[0])
        for kk in writes:
            b = s.bufs.get(kk)
            if b:
                if b[0] is not None:
                    deps.add(b[0])
                for r in b[1]:
                    deps.add(r)
        return deps

    def emit(s, eng, fn, reads=(), writes=(), dma=False):
        reads = [r for r in reads if r is not None]
        rec = Rec(eng, fn)
        idx = len(s.q[eng])
        if TRACE_LINES:
            f_ = sys._getframe(1)
            while f_ is not None and f_.f_code.co_name in ('emit', 'act', 'tt', 'ts', 'stt', 'scan', 'copy', 'memset', 'reduce', 'recip', 'mm', 'tr', 'dma'):
                f_ = f_.f_back
            rec.line = f_.f_lineno if f_ is not None else 0
        deps = s._dep(eng, reads, writes)
        for (e, i) in deps:
            if e == eng and i == idx:
                continue
            if e == eng and eng == "pe":
                continue
            if e == eng and eng == "sp" and s.q[e][i].dma is None:
                continue
            rec.deps.append((e, i))
            s.q[e][i].signal = True
        if dma:
            if eng == "pool":
                k = 16 + (s.npool % 8)
                s.npool += 1
            else:
                k = s.ndma % 16
                s.ndma += 1
            s.dma_uses[k] += 1
            rec.dma = (k, 16 * s.dma_uses[k])
        s.q[eng].append(rec)
        h = (eng, idx)
        for kk in reads:
            s.bufs.setdefault(kk, [None, []])[1].append(h)
        for kk in writes:
            s.bufs[kk] = [h, []]
        return h

    def replay(s, engobjs, sems, dsems):
        for e in ENGS:
            c = 0
            for r in s.q[e]:
                if r.signal and r.dma is None:
                    c += 1
                r.cnt = c

        def run(e, eng):
            waited = {x: 0 for x in ENGS}
            dwaited = [0] * NDMASEM
            for r in s.q[e]:
                for (de, di) in r.deps:
                    dr = s.q[de][di]
                    if dr.dma is not None:
                        k, v = dr.dma
                        if dwaited[k] < v:
                            eng.wait_ge(dsems[k], v)
                            WAITLOG.append((e, 'd%d' % k, v))
                            dwaited[k] = v
                    else:
                        if waited[de] < dr.cnt:
                            eng.wait_ge(sems[de], dr.cnt)
                            WAITLOG.append((e, de, dr.cnt))
                            waited[de] = dr.cnt
                if r.dma is not None:
                    k, v = r.dma
                    if v > 16 and dwaited[k] < v - 16:
                        eng.wait_ge(dsems[k], v - 16)
                        WAITLOG.append((e, 'd%d' % k, v - 16))
                        dwaited[k] = v - 16
                    r.fn(eng).then_inc(dsems[k], 16)
                else:
                    ins = r.fn(eng)
                    if r.signal:
                        ins.then_inc(sems[e], 1)
        return run

    def barrier(s, dummies):
        lasts = [(e, len(s.q[e]) - 1) for e in ENGS if s.q[e]]
        lastd = {}
        for e in ENGS:
            for i, r in enumerate(s.q[e]):
                if r.dma is not None:
                    lastd[r.dma[0]] = (e, i)
        lasts += list(lastd.values())
        for e, fn in dummies.items():
            h = s.emit(e, fn)
            rec = s.q[e][h[1]]
            for (le, li) in lasts:
                if le == e and e == "pe":
                    continue
                if (le, li) == h:
                    continue
                rec.deps.append((le, li))
                s.q[le][li].signal = True

    def keys(s, *vs):
        return [v.key for v in vs if isinstance(v, V)]

    def act(s, out, in_, func, bias=None, scale=None, accum=None, eng="act"):
        kw = {}
        if bias is not None:
            kw['bias'] = bias.ap if isinstance(bias, V) else bias
        if scale is not None:
            kw['scale'] = scale.ap if isinstance(scale, V) else scale
        if accum is not None:
            kw['accum_out'] = accum.ap
        return s.emit("act", lambda e: e.activation(out=out.ap, in_=in_.ap, func=func, **kw),
                      reads=s.keys(in_, bias, scale), writes=s.keys(out, accum))

    def tt(s, eng, out, a, b, op):
        return s.emit(eng, lambda e: e.tensor_tensor(out=out.ap, in0=a.ap, in1=b.ap, op=op),
                      reads=s.keys(a, b), writes=s.keys(out))

    def ts(s, eng, out, a, s1, op0, s2=None, op1=None):
        s1a = s1.ap if isinstance(s1, V) else s1
        s2a = s2.ap if isinstance(s2, V) else s2
        if op1 is None:
            f = lambda e: e.tensor_scalar(out=out.ap, in0=a.ap, scalar1=s1a, scalar2=None, op0=op0)
        else:
            f = lambda e: e.tensor_scalar(out=out.ap, in0=a.ap, scalar1=s1a, scalar2=s2a, op0=op0, op1=op1)
        return s.emit(eng, f, reads=s.keys(a, s1, s2), writes=s.keys(out))

    def stt(s, out, a, sc, b, op0, op1):
        sca = sc.ap if isinstance(sc, V) else sc
        return s.emit("dve", lambda e: e.scalar_tensor_tensor(out=out.ap, in0=a.ap, scalar=sca, in1=b.ap, op0=op0, op1=op1),
                      reads=s.keys(a, sc, b), writes=s.keys(out))

    def scan(s, out, d0, d1, init, op0, op1):
        ia = init.ap if isinstance(init, V) else init
        return s.emit("dve", lambda e: e.tensor_tensor_scan(out=out.ap, data0=d0.ap, data1=d1.ap, initial=ia, op0=op0, op1=op1),
                      reads=s.keys(d0, d1, init), writes=s.keys(out))

    def copy(s, eng, out, in_):
        if eng == "act":
            return s.emit("act", lambda e: e.copy(out=out.ap, in_=in_.ap), reads=s.keys(in_), writes=s.keys(out))
        return s.emit(eng, lambda e: e.tensor_copy(out=out.ap, in_=in_.ap), reads=s.keys(in_), writes=s.keys(out))

    def memset(s, eng, out, val):
        return s.emit(eng, lambda e: e.memset(out.ap, val), writes=s.keys(out))

    def reduce(s, out, in_, op, axis=AX.X):
        return s.emit("dve", lambda e: e.tensor_reduce(out=out.ap, in_=in_.ap, axis=axis, op=op),
                      reads=s.keys(in_), writes=s.keys(out))

    def recip(s, out, in_):
        return s.emit("dve", lambda e: e.reciprocal(out=out.ap, in_=in_.ap), reads=s.keys(in_), writes=s.keys(out))

    def mm(s, out, lhsT, rhs, start=True, stop=True, extra_reads=()):
        return s.emit("pe", lambda e: e.matmul(out.ap, lhsT.ap, rhs.ap, start=start, stop=stop),
                      reads=s.keys(lhsT, rhs) + list(extra_reads), writes=s.keys(out))

    def tr(s, out, in_, ident):
        return s.emit("pe", lambda e: e.transpose(out.ap, in_.ap, ident.ap),
                      reads=s.keys(in_, ident), writes=s.keys(out))

    def dma(s, out, in_, eng="sp", slow=False):
        kw = dict(allow_slow_non_contiguous=True) if slow else {}
        return s.emit(eng, lambda e: e.dma_start(out=out.ap, in_=in_.ap, **kw),
                      reads=s.keys(in_), writes=s.keys(out), dma=True)


def build(NBP, NTP, NSS=4):
    nc = bass.Bass("TRN2", target_bir_lowering=False)
    TP = 128 * NBP * NTP
    din = lambda n, sh, dt=F32: nc.dram_tensor(n, list(sh), dt, kind="ExternalInput").ap()
    dout = lambda n, sh: nc.dram_tensor(n, list(sh), F32, kind="ExternalOutput").ap()
    I = dict(
        xp=din("xp", [TP, D]), xs=din("xs", [NSS, 32, D]),
        wl=din("wl", [2, 128, NTOT]), cpk=din("cpk", [128, 2, CP_N]), hlb=din("hlb", [128, 3, 2]),
        grep=din("grep", [2, 3, D]), gpre=din("gpre", [128, 2, 2, 8]), wg=din("wg", [16, 2, 192]),
        sgla=din("sgla", [2, NSS, 6, 32, 64]), shg=din("shg", [2, NSS, 5, 64, 64]),
        sC=din("sC", [2, NSS, 5, 64, 64]), sn=din("sn", [2, NSS, 5, 64]), sm=din("sm", [2, NSS, 5]),
        cml=din("cml", [2, NSS, 3, 640]), cff=din("cff", [2, NSS, 2, 2 * DFF]),
    )
    O = dict(
        yp=dout("yp", [TP, D]), ys=dout("ys", [NSS, 32, D]),
        pgla=dout("pgla", [2, 1, 6, 32, 64]), phg=dout("phg", [2, 1, 5, 64, 64]), pC=dout("pC", [2, 1, 5, 64, 64]),
        pn=dout("pn", [2, 1, 5, 64]), pm=dout("pm", [2, 1, 5]), pcml=dout("pcml", [2, 1, 3, 640]),
        pcff=dout("pcff", [2, 1, 2, 2 * DFF]),
        ogla=dout("ogla", [2, NSS, 6, 32, 64]), ohg=dout("ohg", [2, NSS, 5, 64, 64]), oC=dout("oC", [2, NSS, 5, 64, 64]),
        on=dout("on", [2, NSS, 5, 64]), om=dout("om", [2, NSS, 5]), ocml=dout("ocml", [2, NSS, 3, 640]),
        ocff=dout("ocff", [2, NSS, 2, 2 * DFF]),
    )
    wscr = nc.dram_tensor("wscr", [2, 128, NTOT], BF16, kind="Internal").ap()
    w0scr = nc.dram_tensor("w0scr", [5, 16], F32, kind="Internal").ap()

    G = Gen(nc)
    NBMAX = max(NBP, 2)
    TMAX = 128 * NBMAX
    NSLOTS = 4
    from contextlib import ExitStack
    es = ExitStack()
    with es:
        def sb(name, shape, dt=F32):
            return V(es.enter_context(nc.sbuf_tensor("sb_" + name, list(shape), dt))[:], name)

        def ps(name, shape, dt=F32):
            return V(es.enter_context(nc.psum_tensor("ps_" + name, list(shape), dt))[:], name)

        ring = sb("ring", [128, NSLOTS * SLOT], BF16)
        x_sb = sb("x", [128, NBMAX, D])
        xn_bf = sb("xnbf", [128, D], BF16)
        actT = sb("actT", [128, 8, TMAX], BF16)
        NTMP = 13
        tmpf = [sb("tf%d" % i, [128, TMAX]) for i in range(NTMP)]
        qt = {n: sb("qt_" + n, [128, TMAX], BF16) for n in
              ["A0", "A1", "B0", "B1", "B2", "C0", "C1", "C2"]}
        kt = {n: sb("kt_" + n, [128, TMAX], BF16) for n in qt}
        HC = 3
        qkpre = {n: sb("pre_" + n, [128, 2 * HC + TMAX]) for n in ["qC0", "qC1", "qC2", "kC0", "kC1", "kC2"]}
        gt = {n: sb("g_" + n, [128, TMAX]) for n in ["ig", "lf", "B", "m", "u", "w", "GT"]}
        small = sb("small", [128, 512])
        v_bf = sb("vbf", [128, NBMAX, 1029], BF16)
        gact = sb("gact", [128, NBMAX, D])
        cat_bf = sb("catbf", [128, D], BF16)
        hbuf = sb("hbuf", [128, D])
        tbuf = sb("tbuf", [128, D])
        HF = 2
        upb = [sb("upb%d" % i, [128, 2 * HF + TMAX]) for i in range(4)]
        yb = [sb("yb%d" % i, [128, TMAX]) for i in range(4)]
        fhist = sb("fhist", [128, 2, 44, 8])
        hT = sb("hT", [128, 22, TMAX], BF16)
        grep_sb = sb("grep", [128, D])
        cpk_sb = sb("cpk", [128, 2, CP_N])
        negb = sb("negb", [128, 2, 4])
        hlb_sb = sb("hlb", [128, 3, 2])
        lbv = sb("lbv", [128, 2, 3, 2])
        gpre_sb = sb("gpre", [128, 2, 2, 8])
        wg_sb = sb("wg", [16, 2, 192])
        ident = sb("ident", [128, 128])
        ident_bf = sb("identbf", [128, 128], BF16)
        mask = sb("mask", [128, 64])
        ones = sb("ones", [128, TMAX])
        selC = sb("selC", [128, 3, 128])
        NSL = 2
        HEADS = [("A", h) for h in range(6)] + [("B", h) for h in range(5)] + [("C", h) for h in range(5)]
        S_H = [[{gh: sb("S%d%d%s%d" % (l, sl, gh[0], gh[1]), [64, 65]) for gh in HEADS} for sl in range(NSL)] for l in range(2)]
        Sbf = {gh: sb("Sbf%s%d" % gh, [64, 65], BF16) for gh in HEADS}
        def hplace(g, h):
            if g == "A":
                return "A%d" % (h // 3), 32 * (h % 3)
            return "%s%d" % (g, h // 2), 64 * (h % 2)
        qrel = {}
        krel = {}
        for gh in HEADS:
            tn_, hb_ = hplace(*gh)
            if hb_ != 0:
                qrel[gh] = sb("qrel%s%d" % gh, [64, TMAX], BF16)
                krel[gh] = sb("krel%s%d" % gh, [64, TMAX], BF16)
        erel = sb("erel", [64, 16, 24])
        PT_par = [sb("PTp%d" % i, [128, 256], BF16) for i in range(4)]
        ktm_par = [sb("ktmp%d" % i, [128, 128], BF16) for i in range(4)]
        stt_r = [sb("sttr%d" % i, [64, 65]) for i in range(4)]
        m_st = [[sb("mst%d%d" % (l, sl), [128, 1]) for sl in range(NSL)] for l in range(2)]
        stt_t = sb("stt_t", [64, 65])
        qhist = sb("qhist", [128, 2, 6, 12])
        w0bc_sb = sb("w0bc", [64, 64])
        gtm_sb = sb("gtm", [128, 69])
        gtm_sb2 = sb("gtm2", [128, 69])

        mmp = [ps("mm%d" % i, [128, 512]) for i in range(2)]
        scp = ps("scp", [128, 512])
        trp = ps("trp", [128, 1024], BF16)
        oP = [ps("oP%d" % i, [128, 512]) for i in range(3)]
        zp = ps("zp", [128, 512])
        mmi = [0]

        def mmbank():
            mmi[0] += 1
            return mmp[mmi[0] % 2]

        G.dma(cpk_sb, V(I['cpk'], "d_cpk"))
        G.dma(hlb_sb, V(I['hlb'], "d_hlb"))
        G.dma(gpre_sb, V(I['gpre'], "d_gpre"))
        G.dma(wg_sb, V(I['wg'], "d_wg"))
        G.memset("pool", ones, 1.0)
        G.memset("pool", ident, 1.0)
        G.emit("pool", lambda e: e.affine_select(out=ident.ap, in_=ident.ap, pattern=[[-1, 128]], compare_op=ALU.is_equal,
                                                 fill=0.0, base=0, channel_multiplier=1), reads=[ident.key], writes=[ident.key])
        G.copy("pool", ident_bf, ident)
        G.memset("pool", mask, 1.0)
        for hh in range(2):
            mv = mask[64 * hh:64 * hh + 64, :]
            G.emit("pool", lambda e, mv=mv: e.affine_select(out=mv.ap, in_=mv.ap, pattern=[[1, 64]], compare_op=ALU.is_ge,
                                                            fill=0.0, base=0, channel_multiplier=-1), reads=[mask.key], writes=[mask.key])
        G.memset("pool", selC, 1.0)
        for t in range(3):
            sv = selC[32:37, t, :]
            G.emit("pool", lambda e, sv=sv, t=t: e.affine_select(out=sv.ap, in_=sv.ap, pattern=[[1, 2], [0, 64]], compare_op=ALU.is_equal,
                                                                 fill=0.0, base=2 * t, channel_multiplier=-1), reads=[selC.key], writes=[selC.key])
        for l in range(2):
            G.ts("pool", negb[:, l, 0:2], cpk_sb[:, l, CP_BG:CP_BG + 2], -1.0, ALU.mult)
            G.ts("pool", negb[:, l, 2:3], cpk_sb[:, l, CP_BF:CP_BF + 1], -1.0, ALU.mult)
        G.memset("pool", lbv[:, 0, :, 0], 0.0)
        G.memset("pool", lbv[:, 0, :, 1], 1.0)
        G.tt("pool", lbv[:, 1, :, 0], hlb_sb[:, :, 1], hlb_sb[:, :, 0], ALU.subtract)
        G.act(lbv[:, 1, :, 0], lbv[:, 1, :, 0], AF.Sigmoid)
        G.ts("pool", lbv[:, 1, :, 1], lbv[:, 1, :, 0], -1.0, ALU.mult, 1.0, ALU.add)
        G.memset("pool", small, 0.0)
        for n in gt:
            G.memset("pool", gt[n], 0.0)
        for b_ in range(NBMAX):
            G.memset("pool", x_sb[:, b_, :].k(b_), 0.0)
        G.memset("pool", qhist, 0.0)
        for n in qkpre:
            G.memset("pool", qkpre[n], 0.0)
        for u in upb:
            G.memset("pool", u, 0.0)
        G.memset("pool", fhist, 0.0)
        for l in range(2):
            for sl in range(NSL):
                G.memset("pool", m_st[l][sl], 0.0)

        Bbuf = [sb("Bb%d" % i, [128, TMAX]) for i in range(9)]
        lnc = sb("lnc", [128, 4])

        def late_init():
            G.memset("pool", bar_t["act"], 0.0)
            G.memset("pool", lnc[:, 0:1], LN8)
            G.memset("pool", lnc[:, 1:2], float(np.log(32.0 ** -0.5)))
            G.memset("pool", lnc[:, 2:3], 1.0)
            G.memset("pool", lnc[:, 3:4], 0.0)
            G.memset("pool", hT, 0.0)
            for n in qt:
                G.memset("pool", qt[n], 0.0)
                G.memset("pool", kt[n], 0.0)
            for t_ in tmpf + Bbuf + yb:
                G.memset("pool", t_, 0.0)
            G.memset("pool", gtm_sb, 0.0)
            G.memset("pool", gtm_sb2, 0.0)
            for i_ in range(4):
                G.memset("pool", PT_par[i_], 0.0)
                G.memset("pool", ktm_par[i_], 0.0)
            G.memset("pool", erel, 0.0)
            G.memset("pool", tbuf, 0.0)
            G.memset("pool", hbuf, 0.0)
            for b_ in range(NBMAX):
                G.memset("pool", gact[:, b_, :].k(b_), 0.0)
                G.memset("pool", v_bf[:, b_, :].k(b_), 0.0)

        bar_t = {e: sb("bar_" + e, [1, 4]) for e in ("act", "dve", "pool")}
        trail_t = sb("trail", [128, 8])

        def trail_hook():
            tr_ = os.environ.get('TRAIL')
            if tr_ in ('pool', 'dve'):
                G.memset(tr_, trail_t, 1.0)
            elif tr_ == 'act':
                G.copy('act', trail_t[:, 0:4], trail_t[:, 4:8])


        def barrier():
            G.barrier({
                "act": lambda e: e.copy(out=bar_t["act"].ap[:, 0:1], in_=bar_t["act"].ap[:, 1:2]),
                "dve": lambda e: e.memset(bar_t["dve"].ap[:, 0:1], 0.0),
                "pool": lambda e: e.memset(bar_t["pool"].ap[:, 0:1], 0.0),
                "pe": lambda e: e.matmul(zp.ap[0:1, 500:501], ident.ap[0:1, 0:1], ident.ap[0:1, 0:1], start=True, stop=True),
            })

        STAGE = int(os.environ.get('KSTAGE', '9'))
        KSUB = int(os.environ.get('KSUB', '99'))
        KC = int(os.environ.get('KC', '99'))
        KW = int(os.environ.get('KW', '99'))
        KL = int(os.environ.get('KL', '99'))
        KG = int(os.environ.get('KG', '3'))
        HS = SLOT // 2
        cnt = 0
        for l in range(2 if STAGE >= 1 else 0):
            for p in PIECES:
                nk, ncl = p['nk'], p['nc']
                kper = max(1, HS // ncl)
                k0 = 0
                while k0 < nk:
                    kn = min(kper, nk - k0)
                    n = kn * ncl
                    o = p['off'] + k0 * ncl
                    sl = cnt % 3
                    cnt += 1
                    st_f = V(ring.ap[:, sl * SLOT:(sl + 1) * SLOT].bitcast(F32)[:, 0:n], ("ring", sl))
                    st_b = V(ring.ap[:, 3 * SLOT + (sl % 2) * HS: 3 * SLOT + (sl % 2) * HS + n], ("ring", 3))
                    G.dma(st_f, V(I['wl'][l, :, o:o + n], "d_wl"))
                    eng = "dve" if cnt % 2 == 0 else "pool"
                    if p['gain'] is None:
                        G.copy(eng, st_b, st_f)
                    else:
                        kc0 = p['kcs'][k0]
                        gv = gpre_sb[:, l, p['gain'], kc0:kc0 + kn]
                        G.tt(eng, st_b.re("p (k c) -> p k c", c=ncl), st_f.re("p (k c) -> p k c", c=ncl),
                             V(gv.ap.rearrange("p (k o) -> p k o", o=1).broadcast_to([128, kn, ncl]), gv.key), ALU.mult)
                    G.dma(V(wscr[l, :, o:o + n], ("wscr", l, p['name'])), st_b, eng="act")
                    k0 += kn

        late_init()
        for _i in range(int(os.environ.get('XDMA', '0'))):
            G.dma(trail_t[:, 0:6], V(I['hlb'].rearrange('p a b -> p (a b)'), 'd_hlb'))
        barrier()
        wslot = [0]

        def wload(l, name):
            p = PIECES[PIDX[name]]
            sl = wslot[0] % NSLOTS
            wslot[0] += 1
            dst = V(ring.ap[:, sl * SLOT: sl * SLOT + p['size']], ("ring", sl))
            G.dma(dst, V(wscr[l, :, p['off']:p['off'] + p['size']], ("wscr", l, name)))
            return dst.re("p (k c) -> p k c", c=p['nc']), p

        def rms_stats(xv, rstd, junk=None):
            G.act(junk if junk is not None else tbuf, xv, AF.Square, accum=rstd)
            G.ts("dve", rstd, rstd, 1.0 / D, ALU.mult, EPS, ALU.add)
            G.act(rstd, rstd, AF.Sqrt)
            G.recip(rstd, rstd)

        def to_fm(NB, normed_src):
            for b in range(NB):
                rstd = (small[:, 0:1] if b % 2 == 0 else small[:, 430:431]).k(("rstd", b % 2))
                G.memset("pool", rstd, 0.0)
                rms_stats(x_sb[:, b, :].k(b), rstd, junk=(tbuf if b % 2 == 0 else hbuf))
                xnb = xn_bf if b % 2 == 0 else cat_bf
                G.ts("dve", xnb, x_sb[:, b, :].k(b), rstd, ALU.mult)
                transpose_block(xnb, b)

        def transpose_block(src_bf, b):
            for kc in range(8):
                G.tr(trp[:, kc * 128:(kc + 1) * 128], src_bf[:, kc * 128:(kc + 1) * 128], ident_bf)
            G.copy("act", actT[:, :, b * 128:(b + 1) * 128].k(b), trp.re("p (k t) -> p k t", t=128))

        def post_norm_residual(l, b, gi):
            rstd = small[:, 1:2].k("rstd2")
            G.memset("pool", rstd, 0.0)
            rms_stats(hbuf, rstd)
            G.stt(tbuf, hbuf, rstd, grep_sb, ALU.mult, ALU.mult)
            G.tt("pool", x_sb[:, b, :].k(b), x_sb[:, b, :].k(b), tbuf, ALU.add)

        def layer_tile(l, NB, nseg, SS, SL, L, slots, first, seg_ids):
            T = 128 * NB
            nchs = SL // L
            nch = nseg * nchs

            def seg3(v, lo=0, n=SL, ext=0):
                return V(v.ap.rearrange("p (s t) -> p s t", t=SS)[:, :, lo:lo + n], v.key)

            def ccol(s, c):
                return s * SS + c * L

            def load_gain(j):
                G.dma(grep_sb, V(I['grep'][l, j:j + 1, :].broadcast_to([128, D]), "d_grep"))
            load_gain(0)
            to_fm(NB, None)
            if KSUB < 1:
                return
            fm_out = {}
            tcount = [0]

            def evac_fm(name, rows, pt):
                if name.startswith("qA") or name.startswith("kA"):
                    G.copy("act", sbA[name][0:rows, 0:T], pt[0:rows, 0:T])
                elif name == "gg":
                    G.copy("act", ggs[0:16, 0:T], pt[0:16, 0:T])
                elif name.startswith("qB"):
                    G.act(sbB[name][0:rows, 0:T], pt[0:rows, 0:T], AF.Silu)
                elif name.startswith("fB"):
                    t = int(name[2])
                    G.act(sbB[name][0:rows, 0:T], pt[0:rows, 0:T], AF.Sigmoid)
                    G.act(sbB["n" + name][0:rows, 0:T], pt[0:rows, 0:T], AF.Sigmoid, scale=-1.0)
                elif name.startswith("qC") or name.startswith("kC"):
                    pre = qkpre[name]
                    ci = (0 if name[0] == "q" else 3) + int(name[2])
                    qh = V(qhist.ap[0:rows, l, ci, :].rearrange("p (s t) -> p s t", t=HC)[:, 0:nseg, :], (qhist.key, l))
                    hcols = V(pre.ap[:, 0:nseg * (HC + SS)].rearrange("p (s t) -> p s t", t=HC + SS)[0:rows, :, 0:HC], pre.key)
                    if first and nseg == 1:
                        G.memset("pool", hcols, 0.0)
                    else:
                        G.copy("pool", hcols, qh)
                    dst = V(pre.ap[:, 0:nseg * (HC + SS)].rearrange("p (s t) -> p s t", t=HC + SS)[0:rows, :, HC:HC + SL], pre.key)
                    src = V(pt.ap[0:rows, 0:T].rearrange("p (s t) -> p s t", t=SS)[:, :, 0:SL], pt.key)
                    G.copy("act", dst, src)
                    G.copy("pool", qh, V(pre.ap[:, 0:nseg * (HC + SS)].rearrange("p (s t) -> p s t", t=HC + SS)[0:rows, :, SL:SL + HC], pre.key))
                elif name == "GI":
                    G.act(gt["ig"][0:69, 0:T], pt[0:69, 0:T], AF.Identity, bias=cpk_sb[0:69, l, CP_BI:CP_BI + 1])
                elif name == "GF":
                    G.act(gt["lf"][0:69, 0:T], pt[0:69, 0:T], AF.Exp, bias=negb[0:69, l, 2:3], scale=-1.0)

            sbA = {"qA0": tmpf[0], "qA1": tmpf[1], "kA0": tmpf[2], "kA1": tmpf[3]}
            fmA = {}
            ggs = tmpf[4]
            sbB = {}
            for t in range(3):
                sbB["qB%d" % t] = Bbuf[3 * t + 0]
                sbB["fB%d" % t] = Bbuf[3 * t + 1]
                sbB["nfB%d" % t] = Bbuf[3 * t + 2]
            for pi in range(5):
                wv, p = wload(l, "fm%d" % pi)
                for (name, c0, rows) in p['tiles']:
                    pt = mmbank()
                    for kc in range(8):
                        G.mm(pt[0:rows, 0:T], wv[:, kc, c0:c0 + rows], actT[:, kc, 0:T].k("all") if False else V(actT.ap[:, kc, 0:T], actT.key),
                             start=(kc == 0), stop=(kc == 7), extra_reads=[(actT.key, b) for b in range(NB)])
                    evac_fm(name, rows, pt)
            if KSUB < 2:
                return
            for pi in range(4):
                wv, p = wload(l, "tm%d" % pi)
                for b in range(NB):
                    pt = mmbank()
                    for kc in range(8):
                        G.mm(pt[:, 0:512], V(actT.ap[:, kc, b * 128:(b + 1) * 128], (actT.key, b)), wv[:, kc, :],
                             start=(kc == 0), stop=(kc == 7))
                    if pi < 2:
                        c0 = pi * 512
                        for (a, bnd, dofs) in ((0, 384, 0), (384, 704, 0)):
                            lo, hi = max(a, c0), min(bnd, c0 + 512)
                            if lo < hi:
                                G.copy("act", v_bf[:, b, lo:hi].k(b), pt[:, lo - c0:hi - c0])
                        lo, hi = max(704, c0), min(1024, c0 + 512)
                        if lo < hi:
                            h0 = (lo - 704) // 64
                            h1 = (hi - 704) // 64
                            dst = V(v_bf.ap[:, b, 704:1029].rearrange("p (h c) -> p h c", c=65)[:, h0:h1, 0:64], (v_bf.key, b))
                            src = V(pt.ap[:, lo - c0:hi - c0].rearrange("p (h c) -> p h c", c=64), pt.key)
                            G.copy("act", dst, src)
                    else:
                        c0 = (pi - 2) * 512
                        lo, hi = c0, min(704, c0 + 512)
                        if lo < hi:
                            G.act(gact[:, b, lo:hi].k(b), pt[:, lo - c0:hi - c0], AF.Silu)
                        lo, hi = max(704, c0), c0 + 512
                        if lo < hi:
                            G.act(gact[:, b, lo:hi].k(b), pt[:, lo - c0:hi - c0], AF.Sigmoid)
            for b in range(NB):
                G.tt("pool", gact[:, b, :].k(b), gact[:, b, :].k(b), grep_sb, ALU.mult)
                ones_col = V(v_bf.ap[:, b, 704:1029].rearrange("p (h c) -> p h c", c=65)[:, :, 64:65], (v_bf.key, b))
                G.memset("pool", ones_col, 1.0)

            if KSUB < 3:
                return
            sarr = {}
            dp_cnt = [0]
            sidx = [8]

            def salloc(n):
                o = sidx[0]
                sidx[0] += (n + 7) // 8 * 8
                return small[:, o:o + n].k("salloc")

            def decay_prep(tn, rows, lraw, gs, qsrc, ksrc, qscale_ln, kscal):
                par_ = dp_cnt[0] % 2
                dp_cnt[0] += 1
                Gb = tmpf[5] if par_ == 0 else tmpf[10]
                Gc = tmpf[6] if par_ == 0 else tmpf[11]
                Ep = tmpf[7] if par_ == 0 else tmpf[12]
                for s in range(nseg):
                    cs = slice(s * SS, s * SS + SL)
                    G.scan(Gb[0:rows, cs], ones[0:rows, cs], lraw[0:rows, cs], 0.0, ALU.mult, ALU.add)
                g3 = V(Gb.ap[0:rows, 0:nseg * SS].rearrange("p (s t) -> p s t", t=SS)[:, :, 0:SL].rearrange("p s (c i) -> p s c i", i=L), Gb.key)
                gm = g3[:, :, :, L // 2 - 1]
                ge = g3[:, :, :, L - 1]
                gb = salloc(nseg * (nchs + 1))
                gbv = V(gb.ap[0:rows].rearrange("p (s c) -> p s c", c=nchs + 1), gb.key)
                G.memset("pool", gb[0:rows], 0.0)
                G.copy("pool", gbv[:, :, 1:nchs + 1], ge)
                dall = salloc(3 * nch)
                d1 = dall[:, 0:nch]; d2 = dall[:, nch:2 * nch]; d12 = dall[:, 2 * nch:3 * nch]
                v3 = lambda a: V(a.ap[0:rows].rearrange("p (s c) -> p s c", c=nchs), a.key)
                G.tt("pool", v3(d1), gm, gbv[:, :, 0:nchs], ALU.subtract)
                G.tt("pool", v3(d2), gbv[:, :, 1:nchs + 1], gm, ALU.subtract)
                G.tt("pool", d12[0:rows], d1[0:rows], d2[0:rows], ALU.add)
                for a in (d1, d2, d12):
                    G.act(a[0:rows], a[0:rows], AF.Exp, scale=gs)
                sarr[tn] = dall
                c4 = V(Gc.ap[0:rows, 0:nseg * SS].rearrange("p (s t) -> p s t", t=SS)[:, :, 0:SL].rearrange("p s (c i) -> p s c i", i=L), Gc.key)
                for s in range(nseg):
                    G.tt("dve", c4[:, s], g3[:, s], V(gm.ap[:, s].rearrange("p (c o) -> p c o", o=1).broadcast_to([rows, nchs, L]), gm.key), ALU.subtract)
                for s in range(nseg):
                    cs = slice(s * SS, s * SS + SL)
                    G.act(Ep[0:rows, cs], Gc[0:rows, cs], AF.Exp, scale=gs, bias=lnc[0:rows, qscale_ln:qscale_ln + 1])
                    G.act(Gc[0:rows, cs], Gc[0:rows, cs], AF.Exp, scale=-gs)
                    G.tt("pool", qt[tn][0:rows, cs], qsrc[0:rows, cs], Ep[0:rows, cs], ALU.mult)
                    if kscal is None:
                        G.tt("pool", kt[tn][0:rows, cs], ksrc[0:rows, cs], Gc[0:rows, cs], ALU.mult)
                    else:
                        G.stt(kt[tn][0:rows, cs], ksrc[0:rows, cs], kscal, Gc[0:rows, cs], ALU.mult, ALU.mult)

            def relocate(tn):
                for gi2, gh in enumerate(HEADS):
                    t2, hb = hplace(*gh)
                    if t2 != tn or hb == 0:
                        continue
                    dkk = 32 if gh[0] == "A" else 64
                    G.dma(qrel[gh][0:dkk, 0:T], qt[tn][hb:hb + dkk, 0:T])
                    G.dma(krel[gh][0:dkk, 0:T], kt[tn][hb:hb + dkk, 0:T])
                    if tn in sarr:
                        G.dma(erel[0:dkk, gi2, 0:3 * nch], sarr[tn][hb:hb + dkk, :], slow=True)

            for t in range(2):
                pt = mmbank()
                G.mm(pt[0:96, 0:T], wg_sb[0:16, l, 96 * t:96 * t + 96], ggs[0:16, 0:T])
                lr = tmpf[8]
                G.act(lr[0:96, 0:T], pt[0:96, 0:T], AF.Exp, scale=-1.0, bias=negb[0:96, l, t:t + 1])
                G.act(lr[0:96, 0:T], lr[0:96, 0:T], AF.Ln, bias=lnc[0:96, 2:3])
                decay_prep("A%d" % t, 96, lr, -1.0 / 16.0, sbA["qA%d" % t], sbA["kA%d" % t], 1, None)
                relocate("A%d" % t)
            for t in range(3):
                rows = BR[t]
                f = sbB["fB%d" % t]
                G.ts("dve", f[0:rows, 0:T], f[0:rows, 0:T], lbv[0:rows, l, t, 1:2], ALU.mult, lbv[0:rows, l, t, 0:1], ALU.add)
                G.act(f[0:rows, 0:T], f[0:rows, 0:T], AF.Ln)
                decay_prep("B%d" % t, rows, f, 1.0, sbB["qB%d" % t], sbB["nfB%d" % t], 0, lbv[0:rows, l, t, 1:2])
                relocate("B%d" % t)

            if KSUB < 4:
                return
            for nm in ("q", "k"):
                for t in range(3):
                    rows = BR[t]
                    name = "%sC%d" % (nm, t)
                    ci = (0 if nm == "q" else 3) + t
                    pre = qkpre[name]
                    p3 = lambda lo: V(pre.ap[:, 0:nseg * (HC + SS)].rearrange("p (s t) -> p s t", t=HC + SS)[0:rows, :, lo:lo + SL], pre.key)
                    y = tmpf[9]
                    y3 = V(y.ap[0:rows, 0:nseg * SS].rearrange("p (s t) -> p s t", t=SS)[:, :, 0:SL], y.key)
                    cw = lambda j: cpk_sb[0:rows, l, CP_CW + 5 * ci + j: CP_CW + 5 * ci + j + 1]
                    G.ts("dve", y3, p3(3), cw(3), ALU.mult, cw(4), ALU.add)
                    for j in range(3):
                        G.stt(y3, p3(j), cw(j), y3, ALU.mult, ALU.add)
                    dstb = (qt if nm == "q" else kt)["C%d" % t]
                    d3 = V(dstb.ap[0:rows, 0:nseg * SS].rearrange("p (s t) -> p s t", t=SS)[:, :, 0:SL], dstb.key)
                    G.act(d3, y3, AF.Silu)
            for t in range(2):
                relocate("C%d" % t)
            if KC < 1:
                return
            ig, lf, Bc, mm_, u_, w_, GT = (gt[n] for n in ["ig", "lf", "B", "m", "u", "w", "GT"])
            R = 69
            G.act(lf[0:R, 0:T], lf[0:R, 0:T], AF.Ln, bias=lnc[0:R, 2:3])
            G.ts("pool", lf[0:R, 0:T], lf[0:R, 0:T], -1.0, ALU.mult)
            for s in range(nseg):
                cs = slice(s * SS, s * SS + SL)
                G.scan(Bc[0:R, cs], ones[0:R, cs], lf[0:R, cs], 0.0, ALU.mult, ALU.add)
                G.scan(mm_[0:R, cs], lf[0:R, cs], ig[0:R, cs], m_st[l][slots[s]][0:R, :], ALU.add, ALU.max)
            if KC < 2:
                return
            b4 = V(Bc.ap[0:R, 0:nseg * SS].rearrange("p (s t) -> p s t", t=SS)[:, :, 0:SL].rearrange("p s (c i) -> p s c i", i=L), Bc.key)
            m4 = V(mm_.ap[0:R, 0:nseg * SS].rearrange("p (s t) -> p s t", t=SS)[:, :, 0:SL].rearrange("p s (c i) -> p s c i", i=L), mm_.key)
            bb = salloc(nseg * (nchs + 1)); mb = salloc(nseg * (nchs + 1)); cc = salloc(nch)
            bbv = V(bb.ap[0:R].rearrange("p (s c) -> p s c", c=nchs + 1), bb.key)
            mbv = V(mb.ap[0:R].rearrange("p (s c) -> p s c", c=nchs + 1), mb.key)
            G.memset("pool", bb[0:R], 0.0)
            G.copy("pool", bbv[:, :, 1:nchs + 1], b4[:, :, :, L - 1])
            G.copy("pool", mbv[:, :, 1:nchs + 1], m4[:, :, :, L - 1])
            for s in range(nseg):
                G.copy("pool", mbv[:, s, 0:1], m_st[l][slots[s]][0:R, :])
            ccv = V(cc.ap[0:R].rearrange("p (s c) -> p s c", c=nchs), cc.key)
            G.tt("pool", ccv, bbv[:, :, 0:nchs], mbv[:, :, 0:nchs], ALU.subtract)
            G.tt("pool", u_[0:R, 0:T], ig[0:R, 0:T], Bc[0:R, 0:T], ALU.subtract)
            G.tt("pool", w_[0:R, 0:T], Bc[0:R, 0:T], mm_[0:R, 0:T], ALU.subtract)
            u4 = V(u_.ap[0:R, 0:nseg * SS].rearrange("p (s t) -> p s t", t=SS)[:, :, 0:SL].rearrange("p s (c i) -> p s c i", i=L), u_.key)
            w4 = V(w_.ap[0:R, 0:nseg * SS].rearrange("p (s t) -> p s t", t=SS)[:, :, 0:SL].rearrange("p s (c i) -> p s c i", i=L), w_.key)
            for s in range(nseg):
                ccb = V(ccv.ap[:, s].rearrange("p (c o) -> p c o", o=1).broadcast_to([R, nchs, L]), cc.key)
                G.tt("dve", u4[:, s], u4[:, s], ccb, ALU.add)
                G.tt("dve", w4[:, s], w4[:, s], ccb, ALU.subtract)
            for s in range(nseg):
                cs = slice(s * SS, s * SS + SL)
                G.act(GT[0:5, cs], u_[0:5, cs], AF.Exp, bias=lnc[0:5, 0:1])
                G.act(GT[32:37, cs], w_[32:37, cs], AF.Exp)
                G.act(GT[64:69, cs], mm_[64:69, cs], AF.Exp, scale=-1.0)
            if KC < 3:
                trail_hook()
                return
            g4 = V(GT.ap[32:37, 0:nseg * SS].rearrange("p (s t) -> p s t", t=SS)[:, :, 0:SL].rearrange("p s (c i) -> p s c i", i=L), GT.key)
            w0l = salloc(nch)
            G.copy(os.environ.get("W0E", "pool"), V(w0l.ap[32:37].rearrange("p (s c) -> p s c", c=nchs), w0l.key), g4[:, :, :, L - 1])
            if KW < 1:
                trail_hook()
                return
            w0bc = w0bc_sb
            G.dma(V(w0scr[:, 0:nch], "w0scr"), w0l[32:37, 0:nch], slow=True)
            for h in range(5):
                G.dma(w0bc[0:64, h * nch:(h + 1) * nch], V(w0scr[h:h + 1, 0:nch].broadcast_to([64, nch]), "w0scr"), slow=True)
            if KC < 4:
                trail_hook()
                return
            for s in range(nseg):
                if os.environ.get('MCOPY', 'pool') != 'skip':
                    DEBUG_H.append(('mcopy', l, G.copy(os.environ.get('MCOPY', 'pool'), m_st[l][slots[s]][0:R, :], mbv[:, s, nchs:nchs + 1])))

            if KSUB < 5:
                return
            groups = [("A", 2, 3, 32, 64, 0), ("B", 3, 2, 64, 64, 384), ("C", 3, 2, 64, 65, 704)]
            gtm_l = [gtm_sb, gtm_sb2]
            stt_i = [0]
            step_i = [0]
            deferred_out = []
            def out_proc(b, gn, ntile, dvx, op_):
                nhg = ntile * (3 if gn == "A" else 2) - (0 if gn == "A" else 1)
                W = nhg * dvx
                o3 = V(op_.ap[:, 0:W].rearrange("p (h c) -> p h c", c=dvx)[:, :, 0:64], op_.key)
                sq3 = V(hbuf.ap[:, 0:nhg * 64].rearrange("p (h c) -> p h c", c=64), hbuf.key)
                G.act(sq3, o3, AF.Square)
                ss = small[:, 2:2 + nhg].k("ss")
                G.reduce(ss, sq3, ALU.add)
                tot = small[:, 400:400 + nhg].k("tot")
                if gn == "C":
                    den = V(op_.ap[:, 0:W].rearrange("p (h c) -> p h c", c=dvx)[:, :, 64], op_.key)
                    gtm = gtm_l[b % 2]
                    w0 = gtm[:, 32:37]
                    em = gtm[:, 64:69]
                    d1_ = small[:, 410:415].k("d1")
                    G.tt("dve", d1_, den, w0, ALU.mult)
                    s2 = small[:, 420:425].k("s2")
                    G.ts("pool", s2, d1_, -1.0, ALU.mult)
                    G.tt("dve", d1_, d1_, s2, ALU.max)
                    G.tt("dve", d1_, d1_, em, ALU.max)
                    G.recip(d1_, d1_)
                    G.tt("pool", d1_, d1_, w0, ALU.mult)
                    s2 = small[:, 420:425].k("s2")
                    G.tt("pool", s2, d1_, d1_, ALU.mult)
                    G.tt("pool", ss, ss, s2, ALU.mult)
                    G.ts("dve", ss, ss, 1.0 / 64, ALU.mult, EPS, ALU.add)
                    G.act(ss, ss, AF.Sqrt)
                    G.recip(ss, ss)
                    G.tt("pool", tot, ss, d1_, ALU.mult)
                else:
                    G.ts("dve", ss, ss, 1.0 / 64, ALU.mult, EPS, ALU.add)
                    G.act(ss, ss, AF.Sqrt)
                    G.recip(tot, ss)
                t3 = V(tbuf.ap[:, 0:nhg * 64].rearrange("p (h c) -> p h c", c=64), tbuf.key)
                G.tt("dve", t3, o3, V(tot.ap.rearrange("p (h o) -> p h o", o=1).broadcast_to([128, nhg, 64]), tot.key), ALU.mult)
                cofs = {"A": 0, "B": 384, "C": 704}[gn]
                G.tt("pool", cat_bf[:, cofs:cofs + nhg * 64], tbuf[:, 0:nhg * 64], V(gact.ap[:, b, cofs:cofs + nhg * 64], (gact.key, b)), ALU.mult)
            for b in range(NB):
                chunks = []
                for s in range(nseg):
                    for c in range(nchs):
                        col = s * SS + c * L
                        if col // 128 == b:
                            chunks.append((s, c, col))
                if KL < 1:
                    return
                gz = zp[:, 256:256 + 69]
                for (s, c, col) in chunks:
                    pb = col % 128
                    G.mm(gz[pb:pb + L, :], GT[0:69, col:col + L], ident[0:69, 0:69])
                gtm = gtm_l[b % 2]
                G.copy("act", gtm[:, 0:69], gz)
                for gi_, (gn, ntile, nh, dk, dvx, vofs) in enumerate(groups[:KG]):
                    if KL < 2:
                        continue
                    op_ = oP[gi_]
                    gofs = {"A": 0, "B": 6, "C": 11}[gn]
                    def step_front(s, c, col, t):
                        pb = col % 128
                        par = pb // 64
                        jr = slice(pb, pb + L)
                        tn = "%s%d" % (gn, t)
                        rows = (96 if gn == "A" else BR[t])
                        nht = rows // dk
                        heads = [(gn, t * nh + h) for h in range(nht)]
                        PTx = PT_par[2 * par + (t % 2)]
                        KTx = ktm_par[2 * par + (t % 2)]
                        stp = step_i[0] % 2
                        step_i[0] += 1
                        scb = scp if stp == 0 else mmp[0]
                        sc = scb[:, 0:nht * L]
                        for h, gh in enumerate(heads):
                            G.mm(sc[jr, h * L:(h + 1) * L], qk(gh, "k")[0:dk, col:col + L], qk(gh, "q")[0:dk, col:col + L])
                        kTp = (V(trp.ap.bitcast(F32), trp.key) if stp == 0 else mmp[1])[:, 0:128]
                        G.mm(kTp[jr, 0:rows], kt[tn][0:rows, col:col + L], ident_bf[0:rows, 0:rows])
                        sc3 = V(sc.ap[jr, 0:nht * L].rearrange("p (h i) -> p h i", i=L), sc.key)
                        pt3 = V(PTx.ap[jr, 0:nht * 64].rearrange("p (h i) -> p h i", i=64)[:, :, 0:L], PTx.key)
                        mk3 = V(mask.ap[jr, 0:L].rearrange("p (o i) -> p o i", o=1).broadcast_to([L, nht, L]), mask.key)
                        if gn == "C":
                            a3 = V(gtm.ap[jr, 2 * t:2 * t + nht].rearrange("p (h o) -> p h o", o=1).broadcast_to([L, nht, L]), gtm.key)
                            tmp3 = V(tbuf.ap[jr, 0:nht * L].rearrange("p (h i) -> p h i", i=L), tbuf.key)
                            G.tt("dve", tmp3, sc3, a3, ALU.mult)
                            G.tt("dve", pt3, tmp3, mk3, ALU.mult)
                            a3k = V(gtm.ap[jr, 2 * t:2 * t + nht].rearrange("p (h o) -> p h o", o=1).broadcast_to([L, nht, dk]), gtm.key)
                            G.tt("dve", V(KTx.ap[jr, 0:rows].rearrange("p (h d) -> p h d", d=dk), KTx.key),
                                 V(kTp.ap[jr, 0:rows].rearrange("p (h d) -> p h d", d=dk), kTp.key), a3k, ALU.mult)
                        else:
                            G.tt("dve", pt3, sc3, mk3, ALU.mult)
                            G.copy("act", KTx[jr, 0:rows], kTp[jr, 0:rows])
                        return (s, c, col, t, jr, heads, PTx, KTx, nht)

                    def qk(gh, which):
                        t2, hb = hplace(*gh)
                        if hb == 0:
                            return (qt if which == "q" else kt)[t2]
                        return (qrel if which == "q" else krel)[gh]

                    def get_ev(gh, ci_):
                        t2, hb = hplace(*gh)
                        if hb == 0:
                            ev = sarr[t2]
                            return (ev[0:dk, ci_:ci_ + 1], ev[0:dk, nch + ci_:nch + ci_ + 1], ev[0:dk, 2 * nch + ci_:2 * nch + ci_ + 1])
                        gi2 = gofs + gh[1]
                        return (erel[0:dk, gi2, ci_:ci_ + 1], erel[0:dk, gi2, nch + ci_:nch + ci_ + 1], erel[0:dk, gi2, 2 * nch + ci_:2 * nch + ci_ + 1])

                    def emit_shadow(gh, s, ci_):
                        S_ = S_H[l][slots[s]][gh]
                        if gn == "C":
                            G.copy("act", Sbf[gh][0:dk, 0:dvx], S_[0:dk, 0:dvx])
                        else:
                            G.act(Sbf[gh][0:dk, 0:dvx], S_[0:dk, 0:dvx], AF.Copy, scale=get_ev(gh, ci_)[0])

                    def step_back(ctx):
                        s, c, col, t, jr, heads, PTx, KTx, nht = ctx
                        ci = s * nchs + c
                        zz = zp[:, 0:nht * dvx]
                        if c == 0:
                            for gh in heads:
                                emit_shadow(gh, s, ci)
                        for h, gh in enumerate(heads):
                            hg = gh[1]
                            vcol = vofs + hg * dvx
                            vv = V(v_bf.ap[:, b, vcol:vcol + dvx], (v_bf.key, b))
                            oo = op_[jr, hg * dvx:(hg + 1) * dvx]
                            G.mm(oo, PTx[:, h * 64:h * 64 + L], vv, start=True, stop=False)
                            G.mm(oo, qk(gh, "q")[0:dk, col:col + L], Sbf[gh][0:dk, 0:dvx], start=False, stop=True)
                        for h, gh in enumerate(heads):
                            hg = gh[1]
                            vcol = vofs + hg * dvx
                            vv = V(v_bf.ap[:, b, vcol:vcol + dvx], (v_bf.key, b))
                            G.mm(zz[0:dk, h * dvx:(h + 1) * dvx], KTx[:, h * dk:(h + 1) * dk], vv)
                        for h, gh in enumerate(heads):
                            hg = gh[1]
                            S = S_H[l][slots[s]][gh]
                            zh = zz[0:dk, h * dvx:(h + 1) * dvx]
                            stt_t = stt_r[stt_i[0] % 4]
                            stt_i[0] += 1
                            if gn == "C":
                                wv_ = w0bc[0:dk, hg * nch + ci: hg * nch + ci + 1]
                                G.tt("dve", stt_t[0:dk, 0:dvx], zh, S[0:dk, 0:dvx], ALU.add)
                                G.act(S[0:dk, 0:dvx], stt_t[0:dk, 0:dvx], AF.Copy, scale=wv_)
                            else:
                                e1, e2, e12 = get_ev(gh, ci)
                                G.ts("dve", stt_t[0:dk, 0:dvx], zh, e2, ALU.mult)
                                G.stt(S[0:dk, 0:dvx], S[0:dk, 0:dvx], e12, stt_t[0:dk, 0:dvx], ALU.mult, ALU.add)
                            if c + 1 < nchs:
                                emit_shadow(gh, s, ci + 1)

                    pend_ = None
                    for (s, c, col) in chunks:
                        for t in range(ntile):
                            ctx_ = step_front(s, c, col, t)
                            if pend_ is not None:
                                step_back(pend_)
                            pend_ = ctx_
                    if pend_ is not None:
                        step_back(pend_)
                    for f_ in deferred_out:
                        f_()
                    del deferred_out[:]
                    deferred_out.append(lambda b=b, gn=gn, ntile=ntile, dvx=dvx, op_=op_: out_proc(b, gn, ntile, dvx, op_))
                deferred_out.append(lambda b=b: transpose_block(cat_bf, b))
            for f_ in deferred_out:
                f_()
            del deferred_out[:]

            if KSUB < 6:
                return
            load_gain(1)
            wo = [wload(l, "out%d" % i)[0] for i in range(2)]
            for b in range(NB):
                for cg in range(2):
                    pt = mmbank()
                    for kc in range(8):
                        G.mm(pt[:, 0:512], V(actT.ap[:, kc, b * 128:(b + 1) * 128], (actT.key, b)), wo[cg][:, kc, :], start=(kc == 0), stop=(kc == 7))
                    G.copy("act", hbuf[:, cg * 512:(cg + 1) * 512], pt[:, 0:512])
                post_norm_residual(l, b, 1)

            if KSUB < 7:
                return
            to_fm(NB, None)
            ffn_pending = [None]

            def ffn_finish(g_, ybs_):
                G.act(ybs_[0][1], ybs_[0][1], AF.Gelu_apprx_tanh)
                h3_ = V(hT.ap[:, g_, 0:nseg * SS].rearrange("p (s t) -> p s t", t=SS)[:, :, 0:SL], (hT.key, g_))
                G.tt("pool", h3_, ybs_[0][1], ybs_[1][1], ALU.mult)

            for pi in range(11):
                wv, p = wload(l, "up%d" % pi)
                for gg_ in range(2):
                    g = 2 * pi + gg_
                    ybs = []
                    for half in range(2):
                        chn = g + 22 * half
                        c0 = gg_ * 256 + half * 128
                        pt = mmbank()
                        for kc in range(8):
                            G.mm(pt[:, 0:T], wv[:, kc, c0:c0 + 128], V(actT.ap[:, kc, 0:T], actT.key), start=(kc == 0), stop=(kc == 7),
                                 extra_reads=[(actT.key, b) for b in range(NB)])
                        ub = upb[(2 * gg_ + half) % 4]
                        u3 = lambda lo, ub=ub: V(ub.ap[:, 0:nseg * (HF + SS)].rearrange("p (s t) -> p s t", t=HF + SS)[:, :, lo:lo + SL], ub.key)
                        fh = V(fhist.ap[:, l, chn, :].rearrange("p (s t) -> p s t", t=2)[:, 0:nseg, :], (fhist.key, l))
                        if not first or nseg > 1:
                            G.copy("pool", u3(0)[:, :, 0:HF], fh)
                        else:
                            G.memset("pool", u3(0)[:, :, 0:HF], 0.0)
                        G.copy("act", u3(HF), V(pt.ap[:, 0:T].rearrange("p (s t) -> p s t", t=SS)[:, :, 0:SL], pt.key))
                        G.copy("pool", fh, V(ub.ap[:, 0:nseg * (HF + SS)].rearrange("p (s t) -> p s t", t=HF + SS)[:, :, SL:SL + HF], ub.key))
                        y = yb[(2 * gg_ + half) % 4]
                        y3 = V(y.ap[:, 0:nseg * SS].rearrange("p (s t) -> p s t", t=SS)[:, :, 0:SL], y.key)
                        fw = lambda j, chn=chn: cpk_sb[:, l, CP_FF + 4 * chn + j: CP_FF + 4 * chn + j + 1]
                        G.ts("dve", y3, u3(2), fw(2), ALU.mult, fw(3), ALU.add)
                        G.stt(y3, u3(1), fw(1), y3, ALU.mult, ALU.add)
                        G.stt(y3, u3(0), fw(0), y3, ALU.mult, ALU.add)
                        ybs.append((y, y3))
                    if ffn_pending[0] is not None:
                        ffn_finish(*ffn_pending[0])
                    ffn_pending[0] = (g, ybs)
            ffn_finish(*ffn_pending[0])
            load_gain(2)
            wd = [[wload(l, "dn%d%d" % (cg, kh))[0] for kh in range(2)] for cg in range(2)]
            for b in range(NB):
                for cg in range(2):
                    pt = mmbank()
                    for kc in range(22):
                        G.mm(pt[:, 0:512], V(hT.ap[:, kc, b * 128:(b + 1) * 128], (hT.key, kc)), wd[cg][kc // 11][:, kc % 11, :],
                             start=(kc == 0), stop=(kc == 21))
                    G.copy("act", hbuf[:, cg * 512:(cg + 1) * 512], pt[:, 0:512])
                post_norm_residual(l, b, 2)

        def zero_states(l, sl):
            for gh in HEADS:
                G.memset("pool", S_H[l][sl][gh], 0.0)
            G.memset("pool", m_st[l][sl], 0.0)

        SRC = {"A": ('sgla', 32), "B": ('shg', 64), "C": ('sC', 64)}

        def load_states(l, sl, sq):
            for gh in HEADS:
                nm, dkk = SRC[gh[0]]
                G.dma(S_H[l][sl][gh][0:dkk, 0:64], V(I[nm][l, sq, gh[1]], "d_st"))
                if gh[0] == "C":
                    G.dma(S_H[l][sl][gh][0:64, 64:65], V(I['sn'][l, sq, gh[1]].rearrange("(k o) -> k o", o=1), "d_st"), slow=True)
            for r0 in (0, 32, 64):
                G.dma(m_st[l][sl][r0:r0 + 5, :], V(I['sm'][l, sq].rearrange("(h o) -> h o", o=1), "d_st"), slow=True)

        def store_states(l, sl, dst, sq):
            og, oh, oc, on_, om_ = dst
            DST = {"A": (og, 32), "B": (oh, 64), "C": (oc, 64)}
            for gh in HEADS:
                d_, dkk = DST[gh[0]]
                G.dma(V(d_[l, sq, gh[1]], ("o_st", next(UNIQ))), S_H[l][sl][gh][0:dkk, 0:64], eng="pool")
                if gh[0] == "C":
                    G.dma(V(on_[l, sq, gh[1]].rearrange("(k o) -> k o", o=1), ("o_st", next(UNIQ))), S_H[l][sl][gh][0:64, 64:65], eng="pool", slow=True)
            G.dma(V(om_[l, sq].rearrange("(h o) -> h o", o=1), ("o_st", next(UNIQ))), m_st[l][sl][0:5, :], eng="pool", slow=True)

        def load_conv(l, seg, sq):
            if os.environ.get('SKIPCONV'):
                return
            for nm, base in (("q", 0), ("k", 320)):
                for t in range(3):
                    rows = BR[t]
                    ci = (0 if nm == "q" else 3) + t
                    dst = V(qhist.ap[0:rows, l, ci, HC * seg:HC * seg + HC], (qhist.key, l))
                    src = V(I['cml'][l, sq, :, base + 128 * t: base + 128 * t + rows].rearrange("j c -> c j"), "d_st")
                    G.dma(dst, src, slow=True)
            for half in range(2):
                for j in range(2):
                    src = V(I['cff'][l, sq, j, half * DFF:(half + 1) * DFF].rearrange("(t p) -> p t", p=128), "d_st")
                    G.dma(V(fhist.ap[:, l, 22 * half:22 * half + 22, 2 * seg + j], (fhist.key, l)), src, slow=True)

        def store_conv(l, seg, dst_ml, dst_ff, sq):
            if os.environ.get('SKIPCONV'):
                return
            for nm, base in (("q", 0), ("k", 320)):
                for t in range(3):
                    rows = BR[t]
                    ci = (0 if nm == "q" else 3) + t
                    src = V(qhist.ap[0:rows, l, ci, HC * seg:HC * seg + HC], (qhist.key, l))
                    dst = V(dst_ml[l, sq, :, base + 128 * t: base + 128 * t + rows].rearrange("j c -> c j"), ("o_st", next(UNIQ)))
                    G.dma(dst, src, eng="pool", slow=True)
            for half in range(2):
                for j in range(2):
                    dst = V(dst_ff[l, sq, j, half * DFF:(half + 1) * DFF].rearrange("(t p) -> p t", p=128), ("o_st", next(UNIQ)))
                    G.dma(dst, V(fhist.ap[:, l, 22 * half:22 * half + 22, 2 * seg + j], (fhist.key, l)), eng="pool", slow=True)

        for l in range(2):
            zero_states(l, 0)
        Tt = 128 * NBP
        for it in range(NTP if STAGE >= 2 else 0):
            for b in range(NBP):
                G.dma(x_sb[:, b, :].k(b), V(I['xp'][it * Tt + b * 128: it * Tt + (b + 1) * 128, :], "d_x"))
            for l in range(2):
                layer_tile(l, NBP, 1, Tt, Tt, 64, [0], it == 0, None)
            for b in range(NBP):
                G.dma(V(O['yp'][it * Tt + b * 128: it * Tt + (b + 1) * 128, :], ("o_y", next(UNIQ))), x_sb[:, b, :].k(b), eng="pool")
        for l in range(2):
            store_states(l, 0, (O['pgla'], O['phg'], O['pC'], O['pn'], O['pm']), 0)
            store_conv(l, 0, O['pcml'], O['pcff'], 0)
        assert NSS == 4
        for i_ in range(4):
            G.memset("pool", PT_par[i_], 0.0)
            G.memset("pool", ktm_par[i_], 0.0)
        for hp in range(2 if STAGE >= 3 else 0):
            G.memset("pool", x_sb[:, 0, :].k(0), 0.0)
            for j in range(2):
                sq = 2 * hp + j
                G.dma(V(x_sb.ap[64 * j:64 * j + 32, 0, :], (x_sb.key, 0)), V(I['xs'][sq], "d_x"))
            for l in range(2):
                for j in range(2):
                    load_states(l, j, 2 * hp + j)
                    load_conv(l, j, 2 * hp + j)
                layer_tile(l, 1, 2, 64, 32, 32, [0, 1], False, None)
                for j in range(2):
                    store_states(l, j, (O['ogla'], O['ohg'], O['oC'], O['on'], O['om']), 2 * hp + j)
                    store_conv(l, j, O['ocml'], O['ocff'], 2 * hp + j)
            for j in range(2):
                sq = 2 * hp + j
                G.dma(V(O['ys'][sq], ("o_y", next(UNIQ))), V(x_sb.ap[64 * j:64 * j + 32, 0, :], (x_sb.key, 0)), eng="pool")
        sems = {e: es.enter_context(nc.semaphore("s_" + e)) for e in ENGS}
        dsems = [es.enter_context(nc.semaphore("d%d" % i)) for i in range(NDMASEM)]
        block = es.enter_context(nc.Block())
        final_waits = {}
        for e in ENGS:
            for r in G.q[e]:
                if r.dma is not None:
                    k, v = r.dma
                    final_waits[k] = max(final_waits.get(k, 0), v)
        run = G.replay(None, sems, dsems)
        global LAST_GEN
        LAST_GEN = G
        engmap = {"pe": "tensor", "act": "scalar", "dve": "vector", "pool": "gpsimd", "sp": "sync"}
        for e in ENGS:
            def body(engobj, e=e):
                run(e, engobj)
                if e == "pool":
                    for k, v in sorted(final_waits.items()):
                        engobj.wait_ge(dsems[k], v)
            getattr(block, engmap[e])(body)
    return nc


def host_layout(inp, c, NSS=4, TP=None):
    f = lambda a: np.ascontiguousarray(np.asarray(a, dtype=np.float32))
    w_in, w_out, w_up, w_dn = f(inp['w_in']), f(inp['w_out']), f(inp['ffn_w_up']), f(inp['ffn_w_down'])
    wl = np.zeros((2, 128, NTOT), np.float32)
    for l in range(2):
        srcs = dict({'in': w_in[l], 'out': w_out[l], 'up': w_up[l], 'dn': w_dn[l]})
        for p in PIECES:
            src = srcs[p['src']]
            arr = np.zeros((128, p['nk'], p['nc']), np.float32)
            valid = p['cols'] >= 0
            for ik, kc in enumerate(p['kcs']):
                arr[:, ik, valid] = src[kc * 128:(kc + 1) * 128, p['cols'][valid]]
            wl[l, :, p['off']:p['off'] + p['size']] = arr.reshape(128, -1)
    cpk = np.zeros((128, 2, CP_N), np.float32)
    hlb = np.zeros((128, 3, 2), np.float32)
    for l in range(2):
        bg = f(inp['gla_b_gate'])[l]
        for t in range(2):
            cpk[0:96, l, CP_BG + t] = bg[96 * t:96 * t + 96]
        cw, cb = f(inp['ml_conv_w'])[l], f(inp['ml_conv_b'])[l]
        for ci in range(6):
            t = ci % 3
            base = (0 if ci < 3 else 320) + 128 * t
            for j in range(4):
                cpk[0:BR[t], l, CP_CW + 5 * ci + j] = cw[j, base:base + BR[t]]
            cpk[0:BR[t], l, CP_CW + 5 * ci + 4] = cb[base:base + BR[t]]
        for r0 in (0, 32, 64):
            cpk[r0:r0 + 5, l, CP_BI] = f(inp['ml_b_i'])[l]
            cpk[r0:r0 + 5, l, CP_BF] = f(inp['ml_b_f'])[l]
        fw, fb = f(inp['ffn_conv_w'])[l], f(inp['ffn_conv_b'])[l]
        for chn in range(44):
            for j in range(3):
                cpk[:, l, CP_FF + 4 * chn + j] = fw[j, chn * 128:(chn + 1) * 128]
            cpk[:, l, CP_FF + 4 * chn + 3] = fb[chn * 128:(chn + 1) * 128]
        lbr = f(inp['hgrn_lb'])[l]
        for t in range(3):
            hlb[0:BR[t], t, l] = lbr[128 * t:128 * t + BR[t]]
    grep = np.stack([f(inp['g_head']), f(inp['g_mix_post']), f(inp['g_ffn_post'])], axis=1)
    gpre = np.zeros((128, 2, 2, 8), np.float32)
    for l in range(2):
        gpre[:, l, 0, :] = f(inp['g_mix_pre'])[l].reshape(8, 128).T
        gpre[:, l, 1, :] = f(inp['g_ffn_pre'])[l].reshape(8, 128).T
    wg = np.ascontiguousarray(f(inp['gla_w_gate']).transpose(1, 0, 2))
    sl = slice(NSS * c, NSS * c + NSS)
    xp = f(inp['x_prompt'])[c]
    if TP is not None:
        xp = xp[:TP]
    return dict(
        xp=np.ascontiguousarray(xp), xs=f(inp['x_sample'])[sl], wl=wl, cpk=cpk, hlb=hlb, grep=np.ascontiguousarray(grep),
        gpre=gpre, wg=wg, sgla=f(inp['state_gla'])[:, sl], shg=f(inp['state_hgrn'])[:, sl], sC=f(inp['state_mlstm_C'])[:, sl],
        sn=f(inp['state_mlstm_n'])[:, sl], sm=f(inp['state_mlstm_m'])[:, sl], cml=f(inp['cache_mlstm_conv'])[:, sl],
        cff=f(inp['cache_ffn_conv'])[:, sl])


NBP_RUN = 2
_NC_CACHE = {}


def kernel(**inputs):
    ncores = 4
    TP = 8192
    NTP = TP // (128 * NBP_RUN)
    key = (NBP_RUN, NTP)
    if key not in _NC_CACHE:
        _NC_CACHE[key] = build(NBP_RUN, NTP)
    nc = _NC_CACHE[key]
    base = host_layout(inputs, 0)
    in_maps = [base]
    for c in range(1, ncores):
        m = host_layout(inputs, c)
        for k in ('wl', 'cpk', 'hlb', 'grep', 'gpre', 'wg'):
            m[k] = base[k]
        in_maps.append(m)
    in_maps = [{k: np.ascontiguousarray(v) for k, v in m.items()} for m in in_maps]
    res = run_bass_kernel_spmd(nc, in_maps, core_ids=[0, 2, 4, 6])
    R = res.results
    cat = lambda k, ax: np.concatenate([np.asarray(r[k], dtype=np.float32) for r in R], axis=ax)
    y_prompt = np.stack([np.asarray(r['yp'], np.float32) for r in R], axis=0)
    y_sample = cat('ys', 0)
    outs = [y_prompt, y_sample]
    for k in ('pgla', 'phg', 'pC', 'pn', 'pm', 'pcml', 'pcff'):
        outs.append(cat(k, 1))
    for k in ('ogla', 'ohg', 'oC', 'on', 'om', 'ocml', 'ocff'):
        outs.append(cat(k, 1))
    return tuple(outs)
```

```python
import os
import sys
import numpy as np
import concourse.bass as bass
import concourse.mybir as mybir
from concourse.bass_utils import run_bass_kernel_spmd

F32 = mybir.dt.float32
BF16 = mybir.dt.bfloat16
AF = mybir.ActivationFunctionType
ALU = mybir.AluOpType
AX = mybir.AxisListType

D = 1024
DFF = 2816
EPS = 1e-6
OFF = dict(gq=0, gk=192, gv=384, gg=768, gr=784, hq=1168, hf=1488, hi=1808, hg=2128,
           mq=2448, mk=2768, mv=3088, mi=3408, mf=3413, mo=3418)
LN8 = float(np.log(0.125))

FM_TILES = []
for t in range(2):
    FM_TILES.append(("qA%d" % t, 96, [(0, OFF['gq'] + 96 * t, 96)]))
for t in range(2):
    FM_TILES.append(("kA%d" % t, 96, [(0, OFF['gk'] + 96 * t, 96)]))
FM_TILES.append(("gg", 16, [(0, OFF['gg'], 16)]))
BR = [128, 128, 64]
for nm, src in (("qB", 'hq'), ("fB", 'hf'), ("qC", 'mq'), ("kC", 'mk')):
    for t in range(3):
        FM_TILES.append(("%s%d" % (nm, t), BR[t], [(0, OFF[src] + 128 * t, BR[t])]))
FM_TILES.append(("GI", 69, [(0, OFF['mi'], 5), (32, OFF['mi'], 5), (64, OFF['mi'], 5)]))
FM_TILES.append(("GF", 69, [(0, OFF['mf'], 5), (32, OFF['mf'], 5), (64, OFF['mf'], 5)]))
FM_PIECES = [["qA0", "qA1", "kA0", "kA1", "gg"], ["qB0", "qB1", "qB2", "fB0"],
             ["fB1", "fB2", "qC0", "qC1"], ["qC2", "kC0", "kC1", "kC2"], ["GI", "GF"]]
FM_BY = {n: (r, s) for n, r, s in FM_TILES}
TM_COLS = (list(range(OFF['gv'], OFF['gv'] + 384)) + list(range(OFF['hi'], OFF['hi'] + 320)) +
           list(range(OFF['mv'], OFF['mv'] + 320)) + list(range(OFF['gr'], OFF['gr'] + 384)) +
           list(range(OFF['hg'], OFF['hg'] + 320)) + list(range(OFF['mo'], OFF['mo'] + 320)))
SLOT = 5632


def piece_table():
    P = []
    for i, names in enumerate(FM_PIECES):
        cols = []
        tiles = []
        for n in names:
            r, srcs = FM_BY[n]
            c = -np.ones(r, np.int64)
            for (dr, sc, nn) in srcs:
                c[dr:dr + nn] = np.arange(sc, sc + nn)
            tiles.append((n, len(cols), r))
            cols += list(c)
        P.append(dict(name="fm%d" % i, src='in', cols=np.array(cols), kcs=list(range(8)), gain=0, tiles=tiles))
    for i in range(4):
        P.append(dict(name="tm%d" % i, src='in', cols=np.array(TM_COLS[512 * i:512 * i + 512]), kcs=list(range(8)), gain=0))
    for i in range(2):
        P.append(dict(name="out%d" % i, src='out', cols=np.arange(512 * i, 512 * i + 512), kcs=list(range(8)), gain=None))
    for i in range(11):
        cols = []
        for g in (2 * i, 2 * i + 1):
            cols += list(range(128 * g, 128 * g + 128)) + list(range(DFF + 128 * g, DFF + 128 * g + 128))
        P.append(dict(name="up%d" % i, src='up', cols=np.array(cols), kcs=list(range(8)), gain=1))
    for cg in range(2):
        for kh in range(2):
            P.append(dict(name="dn%d%d" % (cg, kh), src='dn', cols=np.arange(512 * cg, 512 * cg + 512),
                          kcs=list(range(11 * kh, 11 * kh + 11)), gain=None))
    off = 0
    for p in P:
        p['nk'] = len(p['kcs'])
        p['nc'] = len(p['cols'])
        p['off'] = off
        p['size'] = p['nk'] * p['nc']
        assert p['size'] <= SLOT
        off += p['size']
    return P, off


PIECES, NTOT = piece_table()
PIDX = {p['name']: i for i, p in enumerate(PIECES)}

CP_BG = 0
CP_CW = 2
CP_BI = 32
CP_BF = 33
CP_FF = 34
CP_N = 34 + 176


class V:
    def __init__(s, ap, key):
        s.ap = ap
        s.key = key

    def __getitem__(s, idx):
        return V(s.ap[idx], s.key)

    def k(s, sub):
        return V(s.ap, (s.key, sub))

    def re(s, pat, **kw):
        return V(s.ap.rearrange(pat, **kw), s.key)

    def bc(s, shape):
        return V(s.ap.broadcast_to(list(shape)), s.key)

    def bitcast(s, dt):
        return V(s.ap.bitcast(dt), s.key)


class Rec:
    __slots__ = ("eng", "fn", "deps", "signal", "dma", "cnt", "line")

    def __init__(s, eng, fn):
        s.eng = eng
        s.fn = fn
        s.deps = []
        s.signal = False
        s.dma = None
        s.cnt = 0


ENGS = ["pe", "act", "dve", "pool", "sp"]
DEBUG_H = []
import itertools
UNIQ = itertools.count()
TRACE_LINES = bool(os.environ.get('TRACE_LINES'))
WAITLOG = []
NDMASEM = 24


class Gen:
    def __init__(s, nc):
        s.nc = nc
        s.q = {e: [] for e in ENGS}
        s.bufs = {}
        s.ndma = 0
        s.npool = 0
        s.dma_uses = [0] * NDMASEM

    def _dep(s, eng, reads, writes):
        deps = set()
        for kk in reads:
            b = s.bufs.get(kk)
            if b and b[0] is not None:
                deps.add(b[0])
        for kk in writes:
            b = s.bufs.get(kk)
            if b:
                if b[0] is not None:
                    deps.add(b[0])
                for r in b[1]:
                    deps.add(r)
        return deps

    def emit(s, eng, fn, reads=(), writes=(), dma=False):
        reads = [r for r in reads if r is not None]
        rec = Rec(eng, fn)
        idx = len(s.q[eng])
        if TRACE_LINES:
            f_ = sys._getframe(1)
            while f_ is not None and f_.f_code.co_name in ('emit', 'act', 'tt', 'ts', 'stt', 'scan', 'copy', 'memset', 'reduce', 'recip', 'mm', 'tr', 'dma'):
                f_ = f_.f_back
            rec.line = f_.f_lineno if f_ is not None else 0
        deps = s._dep(eng, reads, writes)
        for (e, i) in deps:
            if e == eng and i == idx:
                continue
            if e == eng and eng == "pe":
                continue
            if e == eng and eng == "sp" and s.q[e][i].dma is None:
                continue
            rec.deps.append((e, i))
            s.q[e][i].signal = True
        if dma:
            if eng == "pool":
                k = 16 + (s.npool % 8)
                s.npool += 1
            else:
                k = s.ndma % 16
                s.ndma += 1
            s.dma_uses[k] += 1
            rec.dma = (k, 16 * s.dma_uses[k])
        s.q[eng].append(rec)
        h = (eng, idx)
        for kk in reads:
            s.bufs.setdefault(kk, [None, []])[1].append(h)
        for kk in writes:
            s.bufs[kk] = [h, []]
        return h

    def replay(s, engobjs, sems, dsems):
        for e in ENGS:
            c = 0
            for r in s.q[e]:
                if r.signal and r.dma is None:
                    c += 1
                r.cnt = c

        def run(e, eng):
            waited = {x: 0 for x in ENGS}
            dwaited = [0] * NDMASEM
            for r in s.q[e]:
                for (de, di) in r.deps:
                    dr = s.q[de][di]
                    if dr.dma is not None:
                        k, v = dr.dma
                        if dwaited[k] < v:
                            eng.wait_ge(dsems[k], v)
                            WAITLOG.append((e, 'd%d' % k, v))
                            dwaited[k] = v
                    else:
                        if waited[de] < dr.cnt:
                            eng.wait_ge(sems[de], dr.cnt)
                            WAITLOG.append((e, de, dr.cnt))
                            waited[de] = dr.cnt
                if r.dma is not None:
                    k, v = r.dma
                    if v > 16 and dwaited[k] < v - 16:
                        eng.wait_ge(dsems[k], v - 16)
                        WAITLOG.append((e, 'd%d' % k, v - 16))
                        dwaited[k] = v - 16
                    r.fn(eng).then_inc(dsems[k], 16)
                else:
                    ins = r.fn(eng)
                    if r.signal:
                        ins.then_inc(sems[e], 1)
        return run

    def barrier(s, dummies):
        lasts = [(e, len(s.q[e]) - 1) for e in ENGS if s.q[e]]
        lastd = {}
        for e in ENGS:
            for i, r in enumerate(s.q[e]):
                if r.dma is not None:
                    lastd[r.dma[0]] = (e, i)
        lasts += list(lastd.values())
        for e, fn in dummies.items():
            h = s.emit(e, fn)
            rec = s.q[e][h[1]]
            for (le, li) in lasts:
                if le == e and e == "pe":
                    continue
                if (le, li) == h:
                    continue
                rec.deps.append((le, li))
                s.q[le][li].signal = True

    def keys(s, *vs):
        return [v.key for v in vs if isinstance(v, V)]

    def act(s, out, in_, func, bias=None, scale=None, accum=None, eng="act"):
        kw = {}
        if bias is not None:
            kw['bias'] = bias.ap if isinstance(bias, V) else bias
        if scale is not None:
            kw['scale'] = scale.ap if isinstance(scale, V) else scale
        if accum is not None:
            kw['accum_out'] = accum.ap
        return s.emit("act", lambda e: e.activation(out=out.ap, in_=in_.ap, func=func, **kw),
                      reads=s.keys(in_, bias, scale), writes=s.keys(out, accum))

    def tt(s, eng, out, a, b, op):
        return s.emit(eng, lambda e: e.tensor_tensor(out=out.ap, in0=a.ap, in1=b.ap, op=op),
                      reads=s.keys(a, b), writes=s.keys(out))

    def ts(s, eng, out, a, s1, op0, s2=None, op1=None):
        s1a = s1.ap if isinstance(s1, V) else s1
        s2a = s2.ap if isinstance(s2, V) else s2
        if op1 is None:
            f = lambda e: e.tensor_scalar(out=out.ap, in0=a.ap, scalar1=s1a, scalar2=None, op0=op0)
        else:
            f = lambda e: e.tensor_scalar(out=out.ap, in0=a.ap, scalar1=s1a, scalar2=s2a, op0=op0, op1=op1)
        return s.emit(eng, f, reads=s.keys(a, s1, s2), writes=s.keys(out))

    def stt(s, out, a, sc, b, op0, op1):
        sca = sc.ap if isinstance(sc, V) else sc
        return s.emit("dve", lambda e: e.scalar_tensor_tensor(out=out.ap, in0=a.ap, scalar=sca, in1=b.ap, op0=op0, op1=op1),
                      reads=s.keys(a, sc, b), writes=s.keys(out))

    def scan(s, out, d0, d1, init, op0, op1):
        ia = init.ap if isinstance(init, V) else init
        return s.emit("dve", lambda e: e.tensor_tensor_scan(out=out.ap, data0=d0.ap, data1=d1.ap, initial=ia, op0=op0, op1=op1),
                      reads=s.keys(d0, d1, init), writes=s.keys(out))

    def copy(s, eng, out, in_):
        if eng == "act":
            return s.emit("act", lambda e: e.copy(out=out.ap, in_=in_.ap), reads=s.keys(in_), writes=s.keys(out))
        return s.emit(eng, lambda e: e.tensor_copy(out=out.ap, in_=in_.ap), reads=s.keys(in_), writes=s.keys(out))

    def memset(s, eng, out, val):
        return s.emit(eng, lambda e: e.memset(out.ap, val), writes=s.keys(out))

    def reduce(s, out, in_, op, axis=AX.X):
        return s.emit("dve", lambda e: e.tensor_reduce(out=out.ap, in_=in_.ap, axis=axis, op=op),
                      reads=s.keys(in_), writes=s.keys(out))

    def recip(s, out, in_):
        return s.emit("dve", lambda e: e.reciprocal(out=out.ap, in_=in_.ap), reads=s.keys(in_), writes=s.keys(out))

    def mm(s, out, lhsT, rhs, start=True, stop=True, extra_reads=()):
        return s.emit("pe", lambda e: e.matmul(out.ap, lhsT.ap, rhs.ap, start=start, stop=stop),
                      reads=s.keys(lhsT, rhs) + list(extra_reads), writes=s.keys(out))

    def tr(s, out, in_, ident):
        return s.emit("pe", lambda e: e.transpose(out.ap, in_.ap, ident.ap),
                      reads=s.keys(in_, ident), writes=s.keys(out))

    def dma(s, out, in_, eng="sp", slow=False):
        kw = dict(allow_slow_non_contiguous=True) if slow else {}
        return s.emit(eng, lambda e: e.dma_start(out=out.ap, in_=in_.ap, **kw),
                      reads=s.keys(in_), writes=s.keys(out), dma=True)


def build(NBP, NTP, NSS=4):
    nc = bass.Bass("TRN2", target_bir_lowering=False)
    TP = 128 * NBP * NTP
    din = lambda n, sh, dt=F32: nc.dram_tensor(n, list(sh), dt, kind="ExternalInput").ap()
    dout = lambda n, sh: nc.dram_tensor(n, list(sh), F32, kind="ExternalOutput").ap()
    I = dict(
        xp=din("xp", [TP, D]), xs=din("xs", [NSS, 32, D]),
        wl=din("wl", [2, 128, NTOT]), cpk=din("cpk", [128, 2, CP_N]), hlb=din("hlb", [128, 3, 2]),
        grep=din("grep", [2, 3, D]), gpre=din("gpre", [128, 2, 2, 8]), wg=din("wg", [16, 2, 192]),
        sgla=din("sgla", [2, NSS, 6, 32, 64]), shg=din("shg", [2, NSS, 5, 64, 64]),
        sC=din("sC", [2, NSS, 5, 64, 64]), sn=din("sn", [2, NSS, 5, 64]), sm=din("sm", [2, NSS, 5]),
        cml=din("cml", [2, NSS, 3, 640]), cff=din("cff", [2, NSS, 2, 2 * DFF]),
    )
    O = dict(
        yp=dout("yp", [TP, D]), ys=dout("ys", [NSS, 32, D]),
        pgla=dout("pgla", [2, 1, 6, 32, 64]), phg=dout("phg", [2, 1, 5, 64, 64]), pC=dout("pC", [2, 1, 5, 64, 64]),
        pn=dout("pn", [2, 1, 5, 64]), pm=dout("pm", [2, 1, 5]), pcml=dout("pcml", [2, 1, 3, 640]),
        pcff=dout("pcff", [2, 1, 2, 2 * DFF]),
        ogla=dout("ogla", [2, NSS, 6, 32, 64]), ohg=dout("ohg", [2, NSS, 5, 64, 64]), oC=dout("oC", [2, NSS, 5, 64, 64]),
        on=dout("on", [2, NSS, 5, 64]), om=dout("om", [2, NSS, 5]), ocml=dout("ocml", [2, NSS, 3, 640]),
        ocff=dout("ocff", [2, NSS, 2, 2 * DFF]),
    )
    wscr = nc.dram_tensor("wscr", [2, 128, NTOT], BF16, kind="Internal").ap()
    w0scr = nc.dram_tensor("w0scr", [5, 16], F32, kind="Internal").ap()

    G = Gen(nc)
    NBMAX = max(NBP, 2)
    TMAX = 128 * NBMAX
    NSLOTS = 4
    from contextlib import ExitStack
    es = ExitStack()
    with es:
        def sb(name, shape, dt=F32):
            return V(es.enter_context(nc.sbuf_tensor("sb_" + name, list(shape), dt))[:], name)

        def ps(name, shape, dt=F32):
            return V(es.enter_context(nc.psum_tensor("ps_" + name, list(shape), dt))[:], name)

        ring = sb("ring", [128, NSLOTS * SLOT], BF16)
        x_sb = sb("x", [128, NBMAX, D])
        xn_bf = sb("xnbf", [128, D], BF16)
        actT = sb("actT", [128, 8, TMAX], BF16)
        NTMP = 13
        tmpf = [sb("tf%d" % i, [128, TMAX]) for i in range(NTMP)]
        qt = {n: sb("qt_" + n, [128, TMAX], BF16) for n in
              ["A0", "A1", "B0", "B1", "B2", "C0", "C1", "C2"]}
        kt = {n: sb("kt_" + n, [128, TMAX], BF16) for n in qt}
        HC = 3
        qkpre = {n: sb("pre_" + n, [128, 4 * HC + TMAX]) for n in ["qC0", "qC1", "qC2", "kC0", "kC1", "kC2"]}
        gt = {n: sb("g_" + n, [128, TMAX]) for n in ["ig", "lf", "B", "m", "u", "w", "GT"]}
        small = sb("small", [128, 512])
        v_bf = sb("vbf", [128, NBMAX, 1029], BF16)
        gact = sb("gact", [128, NBMAX, D])
        cat_bf = sb("catbf", [128, D], BF16)
        hbuf = sb("hbuf", [128, D])
        tbuf = sb("tbuf", [128, D])
        HF = 2
        upb = [sb("upb%d" % i, [128, 4 * HF + TMAX]) for i in range(4)]
        yb = [sb("yb%d" % i, [128, TMAX]) for i in range(4)]
        fhist = sb("fhist", [128, 2, 44, 8])
        hT = sb("hT", [128, 22, TMAX], BF16)
        grep_sb = sb("grep", [128, D])
        cpk_sb = sb("cpk", [128, 2, CP_N])
        negb = sb("negb", [128, 2, 4])
        hlb_sb = sb("hlb", [128, 3, 2])
        lbv = sb("lbv", [128, 2, 3, 2])
        gpre_sb = sb("gpre", [128, 2, 2, 8])
        wg_sb = sb("wg", [16, 2, 192])
        ident = sb("ident", [128, 128])
        ident_bf = sb("identbf", [128, 128], BF16)
        mask = sb("mask", [128, 64])
        ones = sb("ones", [128, TMAX])
        selC = sb("selC", [128, 3, 128])
        NSL = 2
        HEADS = [("A", h) for h in range(6)] + [("B", h) for h in range(5)] + [("C", h) for h in range(5)]
        S_H = [[{gh: sb("S%d%d%s%d" % (l, sl, gh[0], gh[1]), [64, 65]) for gh in HEADS} for sl in range(NSL)] for l in range(2)]
        Sbf = {gh: sb("Sbf%s%d" % gh, [64, 65], BF16) for gh in HEADS}
        def hplace(g, h):
            if g == "A":
                return "A%d" % (h // 3), 32 * (h % 3)
            return "%s%d" % (g, h // 2), 64 * (h % 2)
        qrel = {}
        krel = {}
        for gh in HEADS:
            tn_, hb_ = hplace(*gh)
            if hb_ != 0:
                qrel[gh] = sb("qrel%s%d" % gh, [64, TMAX], BF16)
                krel[gh] = sb("krel%s%d" % gh, [64, TMAX], BF16)
        erel = sb("erel", [64, 16, 24])
        PT_par = [sb("PTp%d" % i, [128, 256], BF16) for i in range(4)]
        ktm_par = [sb("ktmp%d" % i, [128, 128], BF16) for i in range(4)]
        stt_r = [sb("sttr%d" % i, [64, 65]) for i in range(4)]
        m_st = [[sb("mst%d%d" % (l, sl), [128, 1]) for sl in range(NSL)] for l in range(2)]

        def PS(l, sl):
            return (l, sl) if sl < 2 else (1 - l, sl - 2)

        def SH(l, sl):
            a_, b_ = PS(l, sl)
            return S_H[a_][b_]

        def MS(l, sl):
            a_, b_ = PS(l, sl)
            return m_st[a_][b_]
        stt_t = sb("stt_t", [64, 65])
        qhist = sb("qhist", [128, 2, 6, 12])
        w0bc_sb = sb("w0bc", [64, 64])
        gtm_sb = sb("gtm", [128, 69])
        gtm_sb2 = sb("gtm2", [128, 69])

        mmp = [ps("mm%d" % i, [128, 512]) for i in range(2)]
        scp = ps("scp", [128, 512])
        trp = ps("trp", [128, 1024], BF16)
        oP = [ps("oP%d" % i, [128, 512]) for i in range(3)]
        zp = ps("zp", [128, 512])
        mmi = [0]

        def mmbank():
            mmi[0] += 1
            return mmp[mmi[0] % 2]

        G.dma(cpk_sb, V(I['cpk'], "d_cpk"))
        G.dma(hlb_sb, V(I['hlb'], "d_hlb"))
        G.dma(gpre_sb, V(I['gpre'], "d_gpre"))
        G.dma(wg_sb, V(I['wg'], "d_wg"))
        G.memset("pool", ones, 1.0)
        G.memset("pool", ident, 1.0)
        G.emit("pool", lambda e: e.affine_select(out=ident.ap, in_=ident.ap, pattern=[[-1, 128]], compare_op=ALU.is_equal,
                                                 fill=0.0, base=0, channel_multiplier=1), reads=[ident.key], writes=[ident.key])
        G.copy("pool", ident_bf, ident)
        G.memset("pool", mask, 1.0)
        for hh in range(2):
            mv = mask[64 * hh:64 * hh + 64, :]
            G.emit("pool", lambda e, mv=mv: e.affine_select(out=mv.ap, in_=mv.ap, pattern=[[1, 64]], compare_op=ALU.is_ge,
                                                            fill=0.0, base=0, channel_multiplier=-1), reads=[mask.key], writes=[mask.key])
        G.memset("pool", selC, 1.0)
        for t in range(3):
            sv = selC[32:37, t, :]
            G.emit("pool", lambda e, sv=sv, t=t: e.affine_select(out=sv.ap, in_=sv.ap, pattern=[[1, 2], [0, 64]], compare_op=ALU.is_equal,
                                                                 fill=0.0, base=2 * t, channel_multiplier=-1), reads=[selC.key], writes=[selC.key])
        for l in range(2):
            G.ts("pool", negb[:, l, 0:2], cpk_sb[:, l, CP_BG:CP_BG + 2], -1.0, ALU.mult)
            G.ts("pool", negb[:, l, 2:3], cpk_sb[:, l, CP_BF:CP_BF + 1], -1.0, ALU.mult)
        G.memset("pool", lbv[:, 0, :, 0], 0.0)
        G.memset("pool", lbv[:, 0, :, 1], 1.0)
        G.tt("pool", lbv[:, 1, :, 0], hlb_sb[:, :, 1], hlb_sb[:, :, 0], ALU.subtract)
        G.act(lbv[:, 1, :, 0], lbv[:, 1, :, 0], AF.Sigmoid)
        G.ts("pool", lbv[:, 1, :, 1], lbv[:, 1, :, 0], -1.0, ALU.mult, 1.0, ALU.add)
        G.memset("pool", small, 0.0)
        for n in gt:
            G.memset("pool", gt[n], 0.0)
        for b_ in range(NBMAX):
            G.memset("pool", x_sb[:, b_, :].k(b_), 0.0)
        G.memset("pool", qhist, 0.0)
        for n in qkpre:
            G.memset("pool", qkpre[n], 0.0)
        for u in upb:
            G.memset("pool", u, 0.0)
        G.memset("pool", fhist, 0.0)
        for l in range(2):
            for sl in range(NSL):
                G.memset("pool", m_st[l][sl], 0.0)

        Bbuf = [sb("Bb%d" % i, [128, TMAX]) for i in range(9)]
        lnc = sb("lnc", [128, 4])

        def late_init():
            G.memset("pool", bar_t["act"], 0.0)
            G.memset("pool", lnc[:, 0:1], LN8)
            G.memset("pool", lnc[:, 1:2], float(np.log(32.0 ** -0.5)))
            G.memset("pool", lnc[:, 2:3], 1.0)
            G.memset("pool", lnc[:, 3:4], 0.0)
            G.memset("pool", hT, 0.0)
            for n in qt:
                G.memset("pool", qt[n], 0.0)
                G.memset("pool", kt[n], 0.0)
            for t_ in tmpf + Bbuf + yb:
                G.memset("pool", t_, 0.0)
            G.memset("pool", gtm_sb, 0.0)
            G.memset("pool", gtm_sb2, 0.0)
            for i_ in range(4):
                G.memset("pool", PT_par[i_], 0.0)
                G.memset("pool", ktm_par[i_], 0.0)
            G.memset("pool", erel, 0.0)
            G.memset("pool", tbuf, 0.0)
            G.memset("pool", hbuf, 0.0)
            for b_ in range(NBMAX):
                G.memset("pool", gact[:, b_, :].k(b_), 0.0)
                G.memset("pool", v_bf[:, b_, :].k(b_), 0.0)

        bar_t = {e: sb("bar_" + e, [1, 4]) for e in ("act", "dve", "pool")}
        trail_t = sb("trail", [128, 8])

        def trail_hook():
            tr_ = os.environ.get('TRAIL')
            if tr_ in ('pool', 'dve'):
                G.memset(tr_, trail_t, 1.0)
            elif tr_ == 'act':
                G.copy('act', trail_t[:, 0:4], trail_t[:, 4:8])


        def barrier():
            G.barrier({
                "act": lambda e: e.copy(out=bar_t["act"].ap[:, 0:1], in_=bar_t["act"].ap[:, 1:2]),
                "dve": lambda e: e.memset(bar_t["dve"].ap[:, 0:1], 0.0),
                "pool": lambda e: e.memset(bar_t["pool"].ap[:, 0:1], 0.0),
                "pe": lambda e: e.matmul(zp.ap[0:1, 500:501], ident.ap[0:1, 0:1], ident.ap[0:1, 0:1], start=True, stop=True),
            })

        STAGE = int(os.environ.get('KSTAGE', '9'))
        KSUB = int(os.environ.get('KSUB', '99'))
        KC = int(os.environ.get('KC', '99'))
        KW = int(os.environ.get('KW', '99'))
        KL = int(os.environ.get('KL', '99'))
        KG = int(os.environ.get('KG', '3'))
        HS = SLOT // 2
        cnt = 0
        for l in range(2 if STAGE >= 1 else 0):
            for p in PIECES:
                nk, ncl = p['nk'], p['nc']
                kper = max(1, HS // ncl)
                k0 = 0
                while k0 < nk:
                    kn = min(kper, nk - k0)
                    n = kn * ncl
                    o = p['off'] + k0 * ncl
                    sl = cnt % 3
                    cnt += 1
                    st_f = V(ring.ap[:, sl * SLOT:(sl + 1) * SLOT].bitcast(F32)[:, 0:n], ("ring", sl))
                    st_b = V(ring.ap[:, 3 * SLOT + (sl % 2) * HS: 3 * SLOT + (sl % 2) * HS + n], ("ring", 3))
                    G.dma(st_f, V(I['wl'][l, :, o:o + n], "d_wl"))
                    eng = "dve" if cnt % 2 == 0 else "pool"
                    if p['gain'] is None:
                        G.copy(eng, st_b, st_f)
                    else:
                        kc0 = p['kcs'][k0]
                        gv = gpre_sb[:, l, p['gain'], kc0:kc0 + kn]
                        G.tt(eng, st_b.re("p (k c) -> p k c", c=ncl), st_f.re("p (k c) -> p k c", c=ncl),
                             V(gv.ap.rearrange("p (k o) -> p k o", o=1).broadcast_to([128, kn, ncl]), gv.key), ALU.mult)
                    G.dma(V(wscr[l, :, o:o + n], ("wscr", l, p['name'])), st_b, eng="act")
                    k0 += kn

        late_init()
        for _i in range(int(os.environ.get('XDMA', '0'))):
            G.dma(trail_t[:, 0:6], V(I['hlb'].rearrange('p a b -> p (a b)'), 'd_hlb'))
        barrier()
        wslot = [0]

        def wload(l, name):
            p = PIECES[PIDX[name]]
            sl = wslot[0] % NSLOTS
            wslot[0] += 1
            dst = V(ring.ap[:, sl * SLOT: sl * SLOT + p['size']], ("ring", sl))
            G.dma(dst, V(wscr[l, :, p['off']:p['off'] + p['size']], ("wscr", l, name)))
            return dst.re("p (k c) -> p k c", c=p['nc']), p

        def rms_stats(xv, rstd, junk=None):
            G.act(junk if junk is not None else tbuf, xv, AF.Square, accum=rstd)
            G.ts("dve", rstd, rstd, 1.0 / D, ALU.mult, EPS, ALU.add)
            G.act(rstd, rstd, AF.Sqrt)
            G.recip(rstd, rstd)

        def to_fm(NB, normed_src):
            for b in range(NB):
                rstd = (small[:, 0:1] if b % 2 == 0 else small[:, 430:431]).k(("rstd", b % 2))
                G.memset("pool", rstd, 0.0)
                rms_stats(x_sb[:, b, :].k(b), rstd, junk=(tbuf if b % 2 == 0 else hbuf))
                xnb = xn_bf if b % 2 == 0 else cat_bf
                G.ts("dve", xnb, x_sb[:, b, :].k(b), rstd, ALU.mult)
                transpose_block(xnb, b)

        def transpose_block(src_bf, b):
            for kc in range(8):
                G.tr(trp[:, kc * 128:(kc + 1) * 128], src_bf[:, kc * 128:(kc + 1) * 128], ident_bf)
            G.copy("act", actT[:, :, b * 128:(b + 1) * 128].k(b), trp.re("p (k t) -> p k t", t=128))

        def post_norm_residual(l, b, gi):
            rstd = small[:, 1:2].k("rstd2")
            G.memset("pool", rstd, 0.0)
            rms_stats(hbuf, rstd)
            G.stt(tbuf, hbuf, rstd, grep_sb, ALU.mult, ALU.mult)
            G.tt("pool", x_sb[:, b, :].k(b), x_sb[:, b, :].k(b), tbuf, ALU.add)

        def layer_tile(l, NB, nseg, SS, SL, L, slots, first, seg_ids):
            T = 128 * NB
            nchs = SL // L
            nch = nseg * nchs

            def seg3(v, lo=0, n=SL, ext=0):
                return V(v.ap.rearrange("p (s t) -> p s t", t=SS)[:, :, lo:lo + n], v.key)

            def ccol(s, c):
                return s * SS + c * L

            def load_gain(j):
                G.dma(grep_sb, V(I['grep'][l, j:j + 1, :].broadcast_to([128, D]), "d_grep"))
            load_gain(0)
            to_fm(NB, None)
            if KSUB < 1:
                return
            fm_out = {}
            tcount = [0]

            def evac_fm(name, rows, pt):
                if name.startswith("qA") or name.startswith("kA"):
                    G.copy("act", sbA[name][0:rows, 0:T], pt[0:rows, 0:T])
                elif name == "gg":
                    G.copy("act", ggs[0:16, 0:T], pt[0:16, 0:T])
                elif name.startswith("qB"):
                    G.act(sbB[name][0:rows, 0:T], pt[0:rows, 0:T], AF.Silu)
                elif name.startswith("fB"):
                    t = int(name[2])
                    G.act(sbB[name][0:rows, 0:T], pt[0:rows, 0:T], AF.Sigmoid)
                    G.act(sbB["n" + name][0:rows, 0:T], pt[0:rows, 0:T], AF.Sigmoid, scale=-1.0)
                elif name.startswith("qC") or name.startswith("kC"):
                    pre = qkpre[name]
                    ci = (0 if name[0] == "q" else 3) + int(name[2])
                    qh = V(qhist.ap[0:rows, l, ci, :].rearrange("p (s t) -> p s t", t=HC)[:, 0:nseg, :], (qhist.key, l))
                    hcols = V(pre.ap[:, 0:nseg * (HC + SS)].rearrange("p (s t) -> p s t", t=HC + SS)[0:rows, :, 0:HC], pre.key)
                    if first and nseg == 1:
                        G.memset("pool", hcols, 0.0)
                    else:
                        G.copy("pool", hcols, qh)
                    dst = V(pre.ap[:, 0:nseg * (HC + SS)].rearrange("p (s t) -> p s t", t=HC + SS)[0:rows, :, HC:HC + SL], pre.key)
                    src = V(pt.ap[0:rows, 0:T].rearrange("p (s t) -> p s t", t=SS)[:, :, 0:SL], pt.key)
                    G.copy("act", dst, src)
                    G.copy("pool", qh, V(pre.ap[:, 0:nseg * (HC + SS)].rearrange("p (s t) -> p s t", t=HC + SS)[0:rows, :, SL:SL + HC], pre.key))
                elif name == "GI":
                    G.act(gt["ig"][0:69, 0:T], pt[0:69, 0:T], AF.Identity, bias=cpk_sb[0:69, l, CP_BI:CP_BI + 1])
                elif name == "GF":
                    G.act(gt["lf"][0:69, 0:T], pt[0:69, 0:T], AF.Exp, bias=negb[0:69, l, 2:3], scale=-1.0)

            sbA = {"qA0": tmpf[0], "qA1": tmpf[1], "kA0": tmpf[2], "kA1": tmpf[3]}
            fmA = {}
            ggs = tmpf[4]
            sbB = {}
            for t in range(3):
                sbB["qB%d" % t] = Bbuf[3 * t + 0]
                sbB["fB%d" % t] = Bbuf[3 * t + 1]
                sbB["nfB%d" % t] = Bbuf[3 * t + 2]
            for pi in range(5):
                wv, p = wload(l, "fm%d" % pi)
                for (name, c0, rows) in p['tiles']:
                    pt = mmbank()
                    for kc in range(8):
                        G.mm(pt[0:rows, 0:T], wv[:, kc, c0:c0 + rows], actT[:, kc, 0:T].k("all") if False else V(actT.ap[:, kc, 0:T], actT.key),
                             start=(kc == 0), stop=(kc == 7), extra_reads=[(actT.key, b) for b in range(NB)])
                    evac_fm(name, rows, pt)
            if KSUB < 2:
                return
            for pi in range(4):
                wv, p = wload(l, "tm%d" % pi)
                for b in range(NB):
                    pt = mmbank()
                    for kc in range(8):
                        G.mm(pt[:, 0:512], V(actT.ap[:, kc, b * 128:(b + 1) * 128], (actT.key, b)), wv[:, kc, :],
                             start=(kc == 0), stop=(kc == 7))
                    if pi < 2:
                        c0 = pi * 512
                        for (a, bnd, dofs) in ((0, 384, 0), (384, 704, 0)):
                            lo, hi = max(a, c0), min(bnd, c0 + 512)
                            if lo < hi:
                                G.copy("act", v_bf[:, b, lo:hi].k(b), pt[:, lo - c0:hi - c0])
                        lo, hi = max(704, c0), min(1024, c0 + 512)
                        if lo < hi:
                            h0 = (lo - 704) // 64
                            h1 = (hi - 704) // 64
                            dst = V(v_bf.ap[:, b, 704:1029].rearrange("p (h c) -> p h c", c=65)[:, h0:h1, 0:64], (v_bf.key, b))
                            src = V(pt.ap[:, lo - c0:hi - c0].rearrange("p (h c) -> p h c", c=64), pt.key)
                            G.copy("act", dst, src)
                    else:
                        c0 = (pi - 2) * 512
                        lo, hi = c0, min(704, c0 + 512)
                        if lo < hi:
                            G.act(gact[:, b, lo:hi].k(b), pt[:, lo - c0:hi - c0], AF.Silu)
                        lo, hi = max(704, c0), c0 + 512
                        if lo < hi:
                            G.act(gact[:, b, lo:hi].k(b), pt[:, lo - c0:hi - c0], AF.Sigmoid)
            for b in range(NB):
                G.tt("pool", gact[:, b, :].k(b), gact[:, b, :].k(b), grep_sb, ALU.mult)
                ones_col = V(v_bf.ap[:, b, 704:1029].rearrange("p (h c) -> p h c", c=65)[:, :, 64:65], (v_bf.key, b))
                G.memset("pool", ones_col, 1.0)

            if KSUB < 3:
                return
            sarr = {}
            dp_cnt = [0]
            sidx = [8]

            def salloc(n):
                o = sidx[0]
                sidx[0] += (n + 7) // 8 * 8
                return small[:, o:o + n].k("salloc")

            def decay_prep(tn, rows, lraw, gs, qsrc, ksrc, qscale_ln, kscal):
                par_ = dp_cnt[0] % 2
                dp_cnt[0] += 1
                Gb = tmpf[5] if par_ == 0 else tmpf[10]
                Gc = tmpf[6] if par_ == 0 else tmpf[11]
                Ep = tmpf[7] if par_ == 0 else tmpf[12]
                for s in range(nseg):
                    cs = slice(s * SS, s * SS + SL)
                    G.scan(Gb[0:rows, cs], ones[0:rows, cs], lraw[0:rows, cs], 0.0, ALU.mult, ALU.add)
                g3 = V(Gb.ap[0:rows, 0:nseg * SS].rearrange("p (s t) -> p s t", t=SS)[:, :, 0:SL].rearrange("p s (c i) -> p s c i", i=L), Gb.key)
                gm = g3[:, :, :, L // 2 - 1]
                ge = g3[:, :, :, L - 1]
                gb = salloc(nseg * (nchs + 1))
                gbv = V(gb.ap[0:rows].rearrange("p (s c) -> p s c", c=nchs + 1), gb.key)
                G.memset("pool", gb[0:rows], 0.0)
                G.copy("pool", gbv[:, :, 1:nchs + 1], ge)
                dall = salloc(3 * nch)
                d1 = dall[:, 0:nch]; d2 = dall[:, nch:2 * nch]; d12 = dall[:, 2 * nch:3 * nch]
                v3 = lambda a: V(a.ap[0:rows].rearrange("p (s c) -> p s c", c=nchs), a.key)
                G.tt("pool", v3(d1), gm, gbv[:, :, 0:nchs], ALU.subtract)
                G.tt("pool", v3(d2), gbv[:, :, 1:nchs + 1], gm, ALU.subtract)
                G.tt("pool", d12[0:rows], d1[0:rows], d2[0:rows], ALU.add)
                for a in (d1, d2, d12):
                    G.act(a[0:rows], a[0:rows], AF.Exp, scale=gs)
                sarr[tn] = dall
                c4 = V(Gc.ap[0:rows, 0:nseg * SS].rearrange("p (s t) -> p s t", t=SS)[:, :, 0:SL].rearrange("p s (c i) -> p s c i", i=L), Gc.key)
                for s in range(nseg):
                    G.tt("dve", c4[:, s], g3[:, s], V(gm.ap[:, s].rearrange("p (c o) -> p c o", o=1).broadcast_to([rows, nchs, L]), gm.key), ALU.subtract)
                for s in range(nseg):
                    cs = slice(s * SS, s * SS + SL)
                    G.act(Ep[0:rows, cs], Gc[0:rows, cs], AF.Exp, scale=gs, bias=lnc[0:rows, qscale_ln:qscale_ln + 1])
                    G.act(Gc[0:rows, cs], Gc[0:rows, cs], AF.Exp, scale=-gs)
                    G.tt("pool", qt[tn][0:rows, cs], qsrc[0:rows, cs], Ep[0:rows, cs], ALU.mult)
                    if kscal is None:
                        G.tt("pool", kt[tn][0:rows, cs], ksrc[0:rows, cs], Gc[0:rows, cs], ALU.mult)
                    else:
                        G.stt(kt[tn][0:rows, cs], ksrc[0:rows, cs], kscal, Gc[0:rows, cs], ALU.mult, ALU.mult)

            def relocate(tn):
                for gi2, gh in enumerate(HEADS):
                    t2, hb = hplace(*gh)
                    if t2 != tn or hb == 0:
                        continue
                    dkk = 32 if gh[0] == "A" else 64
                    G.dma(qrel[gh][0:dkk, 0:T], qt[tn][hb:hb + dkk, 0:T])
                    G.dma(krel[gh][0:dkk, 0:T], kt[tn][hb:hb + dkk, 0:T])
                    if tn in sarr:
                        G.dma(erel[0:dkk, gi2, 0:3 * nch], sarr[tn][hb:hb + dkk, :], slow=True)

            for t in range(2):
                pt = mmbank()
                G.mm(pt[0:96, 0:T], wg_sb[0:16, l, 96 * t:96 * t + 96], ggs[0:16, 0:T])
                lr = tmpf[8]
                G.act(lr[0:96, 0:T], pt[0:96, 0:T], AF.Exp, scale=-1.0, bias=negb[0:96, l, t:t + 1])
                G.act(lr[0:96, 0:T], lr[0:96, 0:T], AF.Ln, bias=lnc[0:96, 2:3])
                decay_prep("A%d" % t, 96, lr, -1.0 / 16.0, sbA["qA%d" % t], sbA["kA%d" % t], 1, None)
                relocate("A%d" % t)
            for t in range(3):
                rows = BR[t]
                f = sbB["fB%d" % t]
                G.ts("dve", f[0:rows, 0:T], f[0:rows, 0:T], lbv[0:rows, l, t, 1:2], ALU.mult, lbv[0:rows, l, t, 0:1], ALU.add)
                G.act(f[0:rows, 0:T], f[0:rows, 0:T], AF.Ln)
                decay_prep("B%d" % t, rows, f, 1.0, sbB["qB%d" % t], sbB["nfB%d" % t], 0, lbv[0:rows, l, t, 1:2])
                relocate("B%d" % t)

            if KSUB < 4:
                return
            for nm in ("q", "k"):
                for t in range(3):
                    rows = BR[t]
                    name = "%sC%d" % (nm, t)
                    ci = (0 if nm == "q" else 3) + t
                    pre = qkpre[name]
                    p3 = lambda lo: V(pre.ap[:, 0:nseg * (HC + SS)].rearrange("p (s t) -> p s t", t=HC + SS)[0:rows, :, lo:lo + SL], pre.key)
                    y = tmpf[9]
                    y3 = V(y.ap[0:rows, 0:nseg * SS].rearrange("p (s t) -> p s t", t=SS)[:, :, 0:SL], y.key)
                    cw = lambda j: cpk_sb[0:rows, l, CP_CW + 5 * ci + j: CP_CW + 5 * ci + j + 1]
                    G.ts("dve", y3, p3(3), cw(3), ALU.mult, cw(4), ALU.add)
                    for j in range(3):
                        G.stt(y3, p3(j), cw(j), y3, ALU.mult, ALU.add)
                    dstb = (qt if nm == "q" else kt)["C%d" % t]
                    d3 = V(dstb.ap[0:rows, 0:nseg * SS].rearrange("p (s t) -> p s t", t=SS)[:, :, 0:SL], dstb.key)
                    G.act(d3, y3, AF.Silu)
            for t in range(2):
                relocate("C%d" % t)
            if KC < 1:
                return
            ig, lf, Bc, mm_, u_, w_, GT = (gt[n] for n in ["ig", "lf", "B", "m", "u", "w", "GT"])
            R = 69
            G.act(lf[0:R, 0:T], lf[0:R, 0:T], AF.Ln, bias=lnc[0:R, 2:3])
            G.ts("pool", lf[0:R, 0:T], lf[0:R, 0:T], -1.0, ALU.mult)
            for s in range(nseg):
                cs = slice(s * SS, s * SS + SL)
                G.scan(Bc[0:R, cs], ones[0:R, cs], lf[0:R, cs], 0.0, ALU.mult, ALU.add)
                G.scan(mm_[0:R, cs], lf[0:R, cs], ig[0:R, cs], MS(l, slots[s])[0:R, :], ALU.add, ALU.max)
            if KC < 2:
                return
            b4 = V(Bc.ap[0:R, 0:nseg * SS].rearrange("p (s t) -> p s t", t=SS)[:, :, 0:SL].rearrange("p s (c i) -> p s c i", i=L), Bc.key)
            m4 = V(mm_.ap[0:R, 0:nseg * SS].rearrange("p (s t) -> p s t", t=SS)[:, :, 0:SL].rearrange("p s (c i) -> p s c i", i=L), mm_.key)
            bb = salloc(nseg * (nchs + 1)); mb = salloc(nseg * (nchs + 1)); cc = salloc(nch)
            bbv = V(bb.ap[0:R].rearrange("p (s c) -> p s c", c=nchs + 1), bb.key)
            mbv = V(mb.ap[0:R].rearrange("p (s c) -> p s c", c=nchs + 1), mb.key)
            G.memset("pool", bb[0:R], 0.0)
            G.copy("pool", bbv[:, :, 1:nchs + 1], b4[:, :, :, L - 1])
            G.copy("pool", mbv[:, :, 1:nchs + 1], m4[:, :, :, L - 1])
            for s in range(nseg):
                G.copy("pool", mbv[:, s, 0:1], MS(l, slots[s])[0:R, :])
            ccv = V(cc.ap[0:R].rearrange("p (s c) -> p s c", c=nchs), cc.key)
            G.tt("pool", ccv, bbv[:, :, 0:nchs], mbv[:, :, 0:nchs], ALU.subtract)
            G.tt("pool", u_[0:R, 0:T], ig[0:R, 0:T], Bc[0:R, 0:T], ALU.subtract)
            G.tt("pool", w_[0:R, 0:T], Bc[0:R, 0:T], mm_[0:R, 0:T], ALU.subtract)
            u4 = V(u_.ap[0:R, 0:nseg * SS].rearrange("p (s t) -> p s t", t=SS)[:, :, 0:SL].rearrange("p s (c i) -> p s c i", i=L), u_.key)
            w4 = V(w_.ap[0:R, 0:nseg * SS].rearrange("p (s t) -> p s t", t=SS)[:, :, 0:SL].rearrange("p s (c i) -> p s c i", i=L), w_.key)
            for s in range(nseg):
                ccb = V(ccv.ap[:, s].rearrange("p (c o) -> p c o", o=1).broadcast_to([R, nchs, L]), cc.key)
                G.tt("dve", u4[:, s], u4[:, s], ccb, ALU.add)
                G.tt("dve", w4[:, s], w4[:, s], ccb, ALU.subtract)
            for s in range(nseg):
                cs = slice(s * SS, s * SS + SL)
                G.act(GT[0:5, cs], u_[0:5, cs], AF.Exp, bias=lnc[0:5, 0:1])
                G.act(GT[32:37, cs], w_[32:37, cs], AF.Exp)
                G.act(GT[64:69, cs], mm_[64:69, cs], AF.Exp, scale=-1.0)
            if KC < 3:
                trail_hook()
                return
            g4 = V(GT.ap[32:37, 0:nseg * SS].rearrange("p (s t) -> p s t", t=SS)[:, :, 0:SL].rearrange("p s (c i) -> p s c i", i=L), GT.key)
            w0l = salloc(nch)
            G.copy(os.environ.get("W0E", "pool"), V(w0l.ap[32:37].rearrange("p (s c) -> p s c", c=nchs), w0l.key), g4[:, :, :, L - 1])
            if KW < 1:
                trail_hook()
                return
            w0bc = w0bc_sb
            G.dma(V(w0scr[:, 0:nch], "w0scr"), w0l[32:37, 0:nch], slow=True)
            for h in range(5):
                G.dma(w0bc[0:64, h * nch:(h + 1) * nch], V(w0scr[h:h + 1, 0:nch].broadcast_to([64, nch]), "w0scr"), slow=True)
            if KC < 4:
                trail_hook()
                return
            for s in range(nseg):
                if os.environ.get('MCOPY', 'pool') != 'skip':
                    DEBUG_H.append(('mcopy', l, G.copy(os.environ.get('MCOPY', 'pool'), MS(l, slots[s])[0:R, :], mbv[:, s, nchs:nchs + 1])))

            if KSUB < 5:
                return
            groups = [("A", 2, 3, 32, 64, 0), ("B", 3, 2, 64, 64, 384), ("C", 3, 2, 64, 65, 704)]
            gtm_l = [gtm_sb, gtm_sb2]
            stt_i = [0]
            step_i = [0]
            deferred_out = []
            def out_proc(b, gn, ntile, dvx, op_):
                nhg = ntile * (3 if gn == "A" else 2) - (0 if gn == "A" else 1)
                W = nhg * dvx
                o3 = V(op_.ap[:, 0:W].rearrange("p (h c) -> p h c", c=dvx)[:, :, 0:64], op_.key)
                sq3 = V(hbuf.ap[:, 0:nhg * 64].rearrange("p (h c) -> p h c", c=64), hbuf.key)
                G.act(sq3, o3, AF.Square)
                ss = small[:, 2:2 + nhg].k("ss")
                G.reduce(ss, sq3, ALU.add)
                tot = small[:, 400:400 + nhg].k("tot")
                if gn == "C":
                    den = V(op_.ap[:, 0:W].rearrange("p (h c) -> p h c", c=dvx)[:, :, 64], op_.key)
                    gtm = gtm_l[b % 2]
                    w0 = gtm[:, 32:37]
                    em = gtm[:, 64:69]
                    d1_ = small[:, 410:415].k("d1")
                    G.tt("dve", d1_, den, w0, ALU.mult)
                    s2 = small[:, 420:425].k("s2")
                    G.ts("pool", s2, d1_, -1.0, ALU.mult)
                    G.tt("dve", d1_, d1_, s2, ALU.max)
                    G.tt("dve", d1_, d1_, em, ALU.max)
                    G.recip(d1_, d1_)
                    G.tt("pool", d1_, d1_, w0, ALU.mult)
                    s2 = small[:, 420:425].k("s2")
                    G.tt("pool", s2, d1_, d1_, ALU.mult)
                    G.tt("pool", ss, ss, s2, ALU.mult)
                    G.ts("dve", ss, ss, 1.0 / 64, ALU.mult, EPS, ALU.add)
                    G.act(ss, ss, AF.Sqrt)
                    G.recip(ss, ss)
                    G.tt("pool", tot, ss, d1_, ALU.mult)
                else:
                    G.ts("dve", ss, ss, 1.0 / 64, ALU.mult, EPS, ALU.add)
                    G.act(ss, ss, AF.Sqrt)
                    G.recip(tot, ss)
                t3 = V(tbuf.ap[:, 0:nhg * 64].rearrange("p (h c) -> p h c", c=64), tbuf.key)
                G.tt("dve", t3, o3, V(tot.ap.rearrange("p (h o) -> p h o", o=1).broadcast_to([128, nhg, 64]), tot.key), ALU.mult)
                cofs = {"A": 0, "B": 384, "C": 704}[gn]
                G.tt("pool", cat_bf[:, cofs:cofs + nhg * 64], tbuf[:, 0:nhg * 64], V(gact.ap[:, b, cofs:cofs + nhg * 64], (gact.key, b)), ALU.mult)
            for b in range(NB):
                chunks = []
                for s in range(nseg):
                    for c in range(nchs):
                        col = s * SS + c * L
                        if col // 128 == b:
                            chunks.append((s, c, col))
                if KL < 1:
                    return
                gz = zp[:, 256:256 + 69]
                for (s, c, col) in chunks:
                    pb = col % 128
                    G.mm(gz[pb:pb + L, :], GT[0:69, col:col + L], ident[0:69, 0:69])
                gtm = gtm_l[b % 2]
                G.copy("act", gtm[:, 0:69], gz)
                for gi_, (gn, ntile, nh, dk, dvx, vofs) in enumerate(groups[:KG]):
                    if KL < 2:
                        continue
                    op_ = oP[gi_]
                    gofs = {"A": 0, "B": 6, "C": 11}[gn]
                    def step_front(s, c, col, t):
                        pb = col % 128
                        par = pb // 64
                        jr = slice(pb, pb + L)
                        tn = "%s%d" % (gn, t)
                        rows = (96 if gn == "A" else BR[t])
                        nht = rows // dk
                        heads = [(gn, t * nh + h) for h in range(nht)]
                        PTx = PT_par[2 * par + (t % 2)]
                        KTx = ktm_par[2 * par + (t % 2)]
                        stp = step_i[0] % 2
                        step_i[0] += 1
                        scb = scp if stp == 0 else mmp[0]
                        sc = scb[:, 0:nht * L]
                        for h, gh in enumerate(heads):
                            G.mm(sc[jr, h * L:(h + 1) * L], qk(gh, "k")[0:dk, col:col + L], qk(gh, "q")[0:dk, col:col + L])
                        kTp = (V(trp.ap.bitcast(F32), trp.key) if stp == 0 else mmp[1])[:, 0:128]
                        G.mm(kTp[jr, 0:rows], kt[tn][0:rows, col:col + L], ident_bf[0:rows, 0:rows])
                        sc3 = V(sc.ap[jr, 0:nht * L].rearrange("p (h i) -> p h i", i=L), sc.key)
                        pt3 = V(PTx.ap[jr, 0:nht * 64].rearrange("p (h i) -> p h i", i=64)[:, :, 0:L], PTx.key)
                        mk3 = V(mask.ap[jr, 0:L].rearrange("p (o i) -> p o i", o=1).broadcast_to([L, nht, L]), mask.key)
                        if gn == "C":
                            a3 = V(gtm.ap[jr, 2 * t:2 * t + nht].rearrange("p (h o) -> p h o", o=1).broadcast_to([L, nht, L]), gtm.key)
                            tmp3 = V(tbuf.ap[jr, 0:nht * L].rearrange("p (h i) -> p h i", i=L), tbuf.key)
                            G.tt("dve", tmp3, sc3, a3, ALU.mult)
                            G.tt("dve", pt3, tmp3, mk3, ALU.mult)
                            a3k = V(gtm.ap[jr, 2 * t:2 * t + nht].rearrange("p (h o) -> p h o", o=1).broadcast_to([L, nht, dk]), gtm.key)
                            G.tt("dve", V(KTx.ap[jr, 0:rows].rearrange("p (h d) -> p h d", d=dk), KTx.key),
                                 V(kTp.ap[jr, 0:rows].rearrange("p (h d) -> p h d", d=dk), kTp.key), a3k, ALU.mult)
                        else:
                            G.tt("dve", pt3, sc3, mk3, ALU.mult)
                            G.copy("act", KTx[jr, 0:rows], kTp[jr, 0:rows])
                        return (s, c, col, t, jr, heads, PTx, KTx, nht)

                    def qk(gh, which):
                        t2, hb = hplace(*gh)
                        if hb == 0:
                            return (qt if which == "q" else kt)[t2]
                        return (qrel if which == "q" else krel)[gh]

                    def get_ev(gh, ci_):
                        t2, hb = hplace(*gh)
                        if hb == 0:
                            ev = sarr[t2]
                            return (ev[0:dk, ci_:ci_ + 1], ev[0:dk, nch + ci_:nch + ci_ + 1], ev[0:dk, 2 * nch + ci_:2 * nch + ci_ + 1])
                        gi2 = gofs + gh[1]
                        return (erel[0:dk, gi2, ci_:ci_ + 1], erel[0:dk, gi2, nch + ci_:nch + ci_ + 1], erel[0:dk, gi2, 2 * nch + ci_:2 * nch + ci_ + 1])

                    def emit_shadow(gh, s, ci_):
                        S_ = SH(l, slots[s])[gh]
                        if gn == "C":
                            G.copy("act", Sbf[gh][0:dk, 0:dvx], S_[0:dk, 0:dvx])
                        else:
                            G.act(Sbf[gh][0:dk, 0:dvx], S_[0:dk, 0:dvx], AF.Copy, scale=get_ev(gh, ci_)[0])

                    def step_back(ctx):
                        s, c, col, t, jr, heads, PTx, KTx, nht = ctx
                        ci = s * nchs + c
                        zz = zp[:, 0:nht * dvx]
                        if c == 0:
                            for gh in heads:
                                emit_shadow(gh, s, ci)
                        for h, gh in enumerate(heads):
                            hg = gh[1]
                            vcol = vofs + hg * dvx
                            vv = V(v_bf.ap[:, b, vcol:vcol + dvx], (v_bf.key, b))
                            oo = op_[jr, hg * dvx:(hg + 1) * dvx]
                            G.mm(oo, PTx[:, h * 64:h * 64 + L], vv, start=True, stop=False)
                            G.mm(oo, qk(gh, "q")[0:dk, col:col + L], Sbf[gh][0:dk, 0:dvx], start=False, stop=True)
                        for h, gh in enumerate(heads):
                            hg = gh[1]
                            vcol = vofs + hg * dvx
                            vv = V(v_bf.ap[:, b, vcol:vcol + dvx], (v_bf.key, b))
                            G.mm(zz[0:dk, h * dvx:(h + 1) * dvx], KTx[:, h * dk:(h + 1) * dk], vv)
                        for h, gh in enumerate(heads):
                            hg = gh[1]
                            S = SH(l, slots[s])[gh]
                            zh = zz[0:dk, h * dvx:(h + 1) * dvx]
                            stt_t = stt_r[stt_i[0] % 4]
                            stt_i[0] += 1
                            if gn == "C":
                                wv_ = w0bc[0:dk, hg * nch + ci: hg * nch + ci + 1]
                                G.tt("dve", stt_t[0:dk, 0:dvx], zh, S[0:dk, 0:dvx], ALU.add)
                                G.act(S[0:dk, 0:dvx], stt_t[0:dk, 0:dvx], AF.Copy, scale=wv_)
                            else:
                                e1, e2, e12 = get_ev(gh, ci)
                                G.ts("dve", stt_t[0:dk, 0:dvx], zh, e2, ALU.mult)
                                G.stt(S[0:dk, 0:dvx], S[0:dk, 0:dvx], e12, stt_t[0:dk, 0:dvx], ALU.mult, ALU.add)
                            if c + 1 < nchs:
                                emit_shadow(gh, s, ci + 1)

                    pend_ = None
                    for (s, c, col) in chunks:
                        for t in range(ntile):
                            ctx_ = step_front(s, c, col, t)
                            if pend_ is not None:
                                step_back(pend_)
                            pend_ = ctx_
                    if pend_ is not None:
                        step_back(pend_)
                    for f_ in deferred_out:
                        f_()
                    del deferred_out[:]
                    deferred_out.append(lambda b=b, gn=gn, ntile=ntile, dvx=dvx, op_=op_: out_proc(b, gn, ntile, dvx, op_))
                deferred_out.append(lambda b=b: transpose_block(cat_bf, b))
            for f_ in deferred_out:
                f_()
            del deferred_out[:]

            if KSUB < 6:
                return
            load_gain(1)
            wo = [wload(l, "out%d" % i)[0] for i in range(2)]
            for b in range(NB):
                for cg in range(2):
                    pt = mmbank()
                    for kc in range(8):
                        G.mm(pt[:, 0:512], V(actT.ap[:, kc, b * 128:(b + 1) * 128], (actT.key, b)), wo[cg][:, kc, :], start=(kc == 0), stop=(kc == 7))
                    G.copy("act", hbuf[:, cg * 512:(cg + 1) * 512], pt[:, 0:512])
                post_norm_residual(l, b, 1)

            if KSUB < 7:
                return
            to_fm(NB, None)
            ffn_pending = [None]

            def ffn_finish(g_, ybs_):
                G.act(ybs_[0][1], ybs_[0][1], AF.Gelu_apprx_tanh)
                h3_ = V(hT.ap[:, g_, 0:nseg * SS].rearrange("p (s t) -> p s t", t=SS)[:, :, 0:SL], (hT.key, g_))
                G.tt("pool", h3_, ybs_[0][1], ybs_[1][1], ALU.mult)

            for pi in range(11):
                wv, p = wload(l, "up%d" % pi)
                for gg_ in range(2):
                    g = 2 * pi + gg_
                    ybs = []
                    for half in range(2):
                        chn = g + 22 * half
                        c0 = gg_ * 256 + half * 128
                        pt = mmbank()
                        for kc in range(8):
                            G.mm(pt[:, 0:T], wv[:, kc, c0:c0 + 128], V(actT.ap[:, kc, 0:T], actT.key), start=(kc == 0), stop=(kc == 7),
                                 extra_reads=[(actT.key, b) for b in range(NB)])
                        ub = upb[(2 * gg_ + half) % 4]
                        u3 = lambda lo, ub=ub: V(ub.ap[:, 0:nseg * (HF + SS)].rearrange("p (s t) -> p s t", t=HF + SS)[:, :, lo:lo + SL], ub.key)
                        fh = V(fhist.ap[:, l, chn, :].rearrange("p (s t) -> p s t", t=2)[:, 0:nseg, :], (fhist.key, l))
                        if not first or nseg > 1:
                            G.copy("pool", u3(0)[:, :, 0:HF], fh)
                        else:
                            G.memset("pool", u3(0)[:, :, 0:HF], 0.0)
                        G.copy("act", u3(HF), V(pt.ap[:, 0:T].rearrange("p (s t) -> p s t", t=SS)[:, :, 0:SL], pt.key))
                        G.copy("pool", fh, V(ub.ap[:, 0:nseg * (HF + SS)].rearrange("p (s t) -> p s t", t=HF + SS)[:, :, SL:SL + HF], ub.key))
                        y = yb[(2 * gg_ + half) % 4]
                        y3 = V(y.ap[:, 0:nseg * SS].rearrange("p (s t) -> p s t", t=SS)[:, :, 0:SL], y.key)
                        fw = lambda j, chn=chn: cpk_sb[:, l, CP_FF + 4 * chn + j: CP_FF + 4 * chn + j + 1]
                        G.ts("dve", y3, u3(2), fw(2), ALU.mult, fw(3), ALU.add)
                        G.stt(y3, u3(1), fw(1), y3, ALU.mult, ALU.add)
                        G.stt(y3, u3(0), fw(0), y3, ALU.mult, ALU.add)
                        ybs.append((y, y3))
                    if ffn_pending[0] is not None:
                        ffn_finish(*ffn_pending[0])
                    ffn_pending[0] = (g, ybs)
            ffn_finish(*ffn_pending[0])
            load_gain(2)
            wd = [[wload(l, "dn%d%d" % (cg, kh))[0] for kh in range(2)] for cg in range(2)]
            for b in range(NB):
                for cg in range(2):
                    pt = mmbank()
                    for kc in range(22):
                        G.mm(pt[:, 0:512], V(hT.ap[:, kc, b * 128:(b + 1) * 128], (hT.key, kc)), wd[cg][kc // 11][:, kc % 11, :],
                             start=(kc == 0), stop=(kc == 21))
                    G.copy("act", hbuf[:, cg * 512:(cg + 1) * 512], pt[:, 0:512])
                post_norm_residual(l, b, 2)

        def zero_states(l, sl):
            for gh in HEADS:
                G.memset("pool", SH(l, sl)[gh], 0.0)
            G.memset("pool", MS(l, sl), 0.0)

        SRC = {"A": ('sgla', 32), "B": ('shg', 64), "C": ('sC', 64)}

        def load_states(l, sl, sq):
            for gh in HEADS:
                nm, dkk = SRC[gh[0]]
                G.dma(SH(l, sl)[gh][0:dkk, 0:64], V(I[nm][l, sq, gh[1]], "d_st"))
                if gh[0] == "C":
                    G.dma(SH(l, sl)[gh][0:64, 64:65], V(I['sn'][l, sq, gh[1]].rearrange("(k o) -> k o", o=1), "d_st"), slow=True)
            for r0 in (0, 32, 64):
                G.dma(MS(l, sl)[r0:r0 + 5, :], V(I['sm'][l, sq].rearrange("(h o) -> h o", o=1), "d_st"), slow=True)

        def store_states(l, sl, dst, sq):
            og, oh, oc, on_, om_ = dst
            DST = {"A": (og, 32), "B": (oh, 64), "C": (oc, 64)}
            for gh in HEADS:
                d_, dkk = DST[gh[0]]
                G.dma(V(d_[l, sq, gh[1]], ("o_st", next(UNIQ))), SH(l, sl)[gh][0:dkk, 0:64], eng="pool")
                if gh[0] == "C":
                    G.dma(V(on_[l, sq, gh[1]].rearrange("(k o) -> k o", o=1), ("o_st", next(UNIQ))), SH(l, sl)[gh][0:64, 64:65], eng="pool", slow=True)
            G.dma(V(om_[l, sq].rearrange("(h o) -> h o", o=1), ("o_st", next(UNIQ))), MS(l, sl)[0:5, :], eng="pool", slow=True)

        def load_conv(l, seg, sq):
            if os.environ.get('SKIPCONV'):
                return
            for nm, base in (("q", 0), ("k", 320)):
                for t in range(3):
                    rows = BR[t]
                    ci = (0 if nm == "q" else 3) + t
                    dst = V(qhist.ap[0:rows, l, ci, HC * seg:HC * seg + HC], (qhist.key, l))
                    src = V(I['cml'][l, sq, :, base + 128 * t: base + 128 * t + rows].rearrange("j c -> c j"), "d_st")
                    G.dma(dst, src, slow=True)
            for half in range(2):
                for j in range(2):
                    src = V(I['cff'][l, sq, j, half * DFF:(half + 1) * DFF].rearrange("(t p) -> p t", p=128), "d_st")
                    G.dma(V(fhist.ap[:, l, 22 * half:22 * half + 22, 2 * seg + j], (fhist.key, l)), src, slow=True)

        def store_conv(l, seg, dst_ml, dst_ff, sq):
            if os.environ.get('SKIPCONV'):
                return
            for nm, base in (("q", 0), ("k", 320)):
                for t in range(3):
                    rows = BR[t]
                    ci = (0 if nm == "q" else 3) + t
                    src = V(qhist.ap[0:rows, l, ci, HC * seg:HC * seg + HC], (qhist.key, l))
                    dst = V(dst_ml[l, sq, :, base + 128 * t: base + 128 * t + rows].rearrange("j c -> c j"), ("o_st", next(UNIQ)))
                    G.dma(dst, src, eng="pool", slow=True)
            for half in range(2):
                for j in range(2):
                    dst = V(dst_ff[l, sq, j, half * DFF:(half + 1) * DFF].rearrange("(t p) -> p t", p=128), ("o_st", next(UNIQ)))
                    G.dma(dst, V(fhist.ap[:, l, 22 * half:22 * half + 22, 2 * seg + j], (fhist.key, l)), eng="pool", slow=True)

        for l in range(2):
            zero_states(l, 0)
        Tt = 128 * NBP
        for it in range(NTP if STAGE >= 2 else 0):
            for b in range(NBP):
                G.dma(x_sb[:, b, :].k(b), V(I['xp'][it * Tt + b * 128: it * Tt + (b + 1) * 128, :], "d_x"))
            for l in range(2):
                layer_tile(l, NBP, 1, Tt, Tt, 64, [0], it == 0, None)
            for b in range(NBP):
                G.dma(V(O['yp'][it * Tt + b * 128: it * Tt + (b + 1) * 128, :], ("o_y", next(UNIQ))), x_sb[:, b, :].k(b), eng="pool")
        for l in range(2):
            store_states(l, 0, (O['pgla'], O['phg'], O['pC'], O['pn'], O['pm']), 0)
            store_conv(l, 0, O['pcml'], O['pcff'], 0)
        assert NSS == 4
        for i_ in range(4):
            G.memset("pool", PT_par[i_], 0.0)
            G.memset("pool", ktm_par[i_], 0.0)
        if STAGE >= 3:
            for b_ in range(2):
                G.memset("pool", x_sb[:, b_, :].k(b_), 0.0)
            for sq in range(4):
                b_, pb_ = sq // 2, 64 * (sq % 2)
                G.dma(V(x_sb.ap[pb_:pb_ + 32, b_, :], (x_sb.key, b_)), V(I['xs'][sq], "d_x"))
            for l in range(2):
                for sq in range(4):
                    load_states(l, sq, sq)
                    load_conv(l, sq, sq)
                layer_tile(l, 2, 4, 64, 32, 32, [0, 1, 2, 3], False, None)
                for sq in range(4):
                    store_states(l, sq, (O['ogla'], O['ohg'], O['oC'], O['on'], O['om']), sq)
                    store_conv(l, sq, O['ocml'], O['ocff'], sq)
            for sq in range(4):
                b_, pb_ = sq // 2, 64 * (sq % 2)
                G.dma(V(O['ys'][sq], ("o_y", next(UNIQ))), V(x_sb.ap[pb_:pb_ + 32, b_, :], (x_sb.key, b_)), eng="pool")
        sems = {e: es.enter_context(nc.semaphore("s_" + e)) for e in ENGS}
        dsems = [es.enter_context(nc.semaphore("d%d" % i)) for i in range(NDMASEM)]
        block = es.enter_context(nc.Block())
        final_waits = {}
        for e in ENGS:
            for r in G.q[e]:
                if r.dma is not None:
                    k, v = r.dma
                    final_waits[k] = max(final_waits.get(k, 0), v)
        run = G.replay(None, sems, dsems)
        global LAST_GEN
        LAST_GEN = G
        engmap = {"pe": "tensor", "act": "scalar", "dve": "vector", "pool": "gpsimd", "sp": "sync"}
        for e in ENGS:
            def body(engobj, e=e):
                run(e, engobj)
                if e == "pool":
                    for k, v in sorted(final_waits.items()):
                        engobj.wait_ge(dsems[k], v)
            getattr(block, engmap[e])(body)
    return nc


def host_layout(inp, c, NSS=4, TP=None):
    f = lambda a: np.ascontiguousarray(np.asarray(a, dtype=np.float32))
    w_in, w_out, w_up, w_dn = f(inp['w_in']), f(inp['w_out']), f(inp['ffn_w_up']), f(inp['ffn_w_down'])
    wl = np.zeros((2, 128, NTOT), np.float32)
    for l in range(2):
        srcs = dict({'in': w_in[l], 'out': w_out[l], 'up': w_up[l], 'dn': w_dn[l]})
        for p in PIECES:
            src = srcs[p['src']]
            arr = np.zeros((128, p['nk'], p['nc']), np.float32)
            valid = p['cols'] >= 0
            for ik, kc in enumerate(p['kcs']):
                arr[:, ik, valid] = src[kc * 128:(kc + 1) * 128, p['cols'][valid]]
            wl[l, :, p['off']:p['off'] + p['size']] = arr.reshape(128, -1)
    cpk = np.zeros((128, 2, CP_N), np.float32)
    hlb = np.zeros((128, 3, 2), np.float32)
    for l in range(2):
        bg = f(inp['gla_b_gate'])[l]
        for t in range(2):
            cpk[0:96, l, CP_BG + t] = bg[96 * t:96 * t + 96]
        cw, cb = f(inp['ml_conv_w'])[l], f(inp['ml_conv_b'])[l]
        for ci in range(6):
            t = ci % 3
            base = (0 if ci < 3 else 320) + 128 * t
            for j in range(4):
                cpk[0:BR[t], l, CP_CW + 5 * ci + j] = cw[j, base:base + BR[t]]
            cpk[0:BR[t], l, CP_CW + 5 * ci + 4] = cb[base:base + BR[t]]
        for r0 in (0, 32, 64):
            cpk[r0:r0 + 5, l, CP_BI] = f(inp['ml_b_i'])[l]
            cpk[r0:r0 + 5, l, CP_BF] = f(inp['ml_b_f'])[l]
        fw, fb = f(inp['ffn_conv_w'])[l], f(inp['ffn_conv_b'])[l]
        for chn in range(44):
            for j in range(3):
                cpk[:, l, CP_FF + 4 * chn + j] = fw[j, chn * 128:(chn + 1) * 128]
            cpk[:, l, CP_FF + 4 * chn + 3] = fb[chn * 128:(chn + 1) * 128]
        lbr = f(inp['hgrn_lb'])[l]
        for t in range(3):
            hlb[0:BR[t], t, l] = lbr[128 * t:128 * t + BR[t]]
    grep = np.stack([f(inp['g_head']), f(inp['g_mix_post']), f(inp['g_ffn_post'])], axis=1)
    gpre = np.zeros((128, 2, 2, 8), np.float32)
    for l in range(2):
        gpre[:, l, 0, :] = f(inp['g_mix_pre'])[l].reshape(8, 128).T
        gpre[:, l, 1, :] = f(inp['g_ffn_pre'])[l].reshape(8, 128).T
    wg = np.ascontiguousarray(f(inp['gla_w_gate']).transpose(1, 0, 2))
    sl = slice(NSS * c, NSS * c + NSS)
    xp = f(inp['x_prompt'])[c]
    if TP is not None:
        xp = xp[:TP]
    return dict(
        xp=np.ascontiguousarray(xp), xs=f(inp['x_sample'])[sl], wl=wl, cpk=cpk, hlb=hlb, grep=np.ascontiguousarray(grep),
        gpre=gpre, wg=wg, sgla=f(inp['state_gla'])[:, sl], shg=f(inp['state_hgrn'])[:, sl], sC=f(inp['state_mlstm_C'])[:, sl],
        sn=f(inp['state_mlstm_n'])[:, sl], sm=f(inp['state_mlstm_m'])[:, sl], cml=f(inp['cache_mlstm_conv'])[:, sl],
        cff=f(inp['cache_ffn_conv'])[:, sl])


NBP_RUN = 2
_NC_CACHE = {}


def kernel(**inputs):
    ncores = 4
    TP = 8192
    NTP = TP // (128 * NBP_RUN)
    key = (NBP_RUN, NTP)
    if key not in _NC_CACHE:
        _NC_CACHE[key] = build(NBP_RUN, NTP)
    nc = _NC_CACHE[key]
    base = host_layout(inputs, 0)
    in_maps = [base]
    for c in range(1, ncores):
        m = host_layout(inputs, c)
        for k in ('wl', 'cpk', 'hlb', 'grep', 'gpre', 'wg'):
            m[k] = base[k]
        in_maps.append(m)
    in_maps = [{k: np.ascontiguousarray(v) for k, v in m.items()} for m in in_maps]
    res = run_bass_kernel_spmd(nc, in_maps, core_ids=list(range(ncores)))
    R = res.results
    cat = lambda k, ax: np.concatenate([np.asarray(r[k], dtype=np.float32) for r in R], axis=ax)
    y_prompt = np.stack([np.asarray(r['yp'], np.float32) for r in R], axis=0)
    y_sample = cat('ys', 0)
    outs = [y_prompt, y_sample]
    for k in ('pgla', 'phg', 'pC', 'pn', 'pm', 'pcml', 'pcff'):
        outs.append(cat(k, 1))
    for k in ('ogla', 'ohg', 'oC', 'on', 'om', 'ocml', 'ocff'):
        outs.append(cat(k, 1))
    return tuple(outs)
```
